# Optimizing a Trainium2 kernel written in Bass

```python
import jax
import jax.numpy as jnp
from jax import lax
import numpy as np

D_MODEL = 1024
BATCH = 8
SEQ = 2048
DEPTH = 2
DEC_BATCH = 128
DEC_SEQ = 4
PAST_LEN = 16384
PAGE_SIZE = 128

D_MIX = D_MODEL
HG_HEADS = 4
HG_DK = D_MIX // 16
HG_DV = D_MIX // 16
HG_WIDTH = HG_HEADS * HG_DV
HG_FDIM = HG_HEADS * HG_DK
GDN_HEADS = 4
GDN_DK = D_MIX // 8
GDN_DV = D_MIX // 8
GDN_WIDTH = GDN_HEADS * GDN_DV
GDN_QKV = 2 * GDN_HEADS * GDN_DK + GDN_WIDTH
LRU_WIDTH = D_MIX - HG_WIDTH - GDN_WIDTH
LRU_BLOCKS = 4
LRU_BLOCK = LRU_WIDTH // LRU_BLOCKS
LRU_C = 8.0
CONV_QKV = 4
CONV_LRU = 4
CONV_FFN = 3
D_FF = 128 * ((8 * D_MODEL // 3 + 127) // 128)
CHUNK = 64
EPS = 1e-6
IN_SPLITS = (HG_FDIM, HG_FDIM, HG_WIDTH, HG_WIDTH,
             GDN_QKV, GDN_WIDTH, GDN_HEADS, GDN_HEADS,
             LRU_WIDTH, LRU_WIDTH)
D_IN = sum(IN_SPLITS)

kernel_name = "hymba_style_hgrn2_gdn_rglru_convffn_step"


def _rmsnorm(x, g):
    xf = x.astype(jnp.float32)
    y = xf * lax.rsqrt(jnp.mean(xf * xf, axis=-1, keepdims=True) + EPS)
    return (y * g.astype(jnp.float32)).astype(x.dtype)


def _l2norm(x):
    return x * lax.rsqrt(jnp.sum(x * x, axis=-1, keepdims=True) + EPS)


def _causal_dwconv(x, prev, w):
    K = w.shape[0]
    T = x.shape[1]
    xp = jnp.concatenate([prev.astype(x.dtype), x], axis=1)
    y = xp[:, 0:T] * w[0]
    for j in range(1, K):
        y = y + xp[:, j:j + T] * w[j]
    return y, xp[:, T:]


def _to_chunks(a, C):
    T = a.shape[2]
    N = -(-T // C)
    a = jnp.pad(a, [(0, 0), (0, 0), (0, N * C - T)] + [(0, 0)] * (a.ndim - 3))
    a = a.reshape(a.shape[:2] + (N, C) + a.shape[3:])
    return jnp.moveaxis(a, 2, 0)


def _from_chunks(o, T):
    o = jnp.moveaxis(o, 0, 2)
    o = o.reshape(o.shape[:2] + (-1, o.shape[-1]))[:, :, :T]
    return jnp.swapaxes(o, 1, 2)


def _hgrn2(q, k, v, logf, S0):
    T = q.shape[1]
    C = min(CHUNK, T)
    xs = tuple(_to_chunks(jnp.swapaxes(a, 1, 2), C) for a in (q, k, v, logf))
    incl = jnp.tril(jnp.ones((C, C), bool))[:, :, None]

    def step(S, inp):
        qc, kc, vc, lc = inp
        G = jnp.cumsum(lc, axis=2)
        diff = G[:, :, :, None, :] - G[:, :, None, :, :]
        decay = jnp.where(incl, jnp.exp(jnp.where(incl, diff, 0.0)), 0.0)
        A = jnp.einsum('bhtsd,bhsd->bhts', qc[:, :, :, None, :] * decay, kc)
        o = (jnp.einsum('bhtd,bhde->bhte', qc * jnp.exp(G), S)
             + jnp.einsum('bhts,bhse->bhte', A, vc))
        Gl = G[:, :, -1]
        S = (S * jnp.exp(Gl)[..., None]
             + jnp.einsum('bhsd,bhse->bhde', kc * jnp.exp(Gl[:, :, None] - G), vc))
        return S, o

    S, o = lax.scan(step, S0, xs)
    return _from_chunks(o, T), S


def _gated_delta(q, k, v, g, beta, S0):
    T = q.shape[1]
    C = min(CHUNK, T)
    xs = tuple(_to_chunks(jnp.swapaxes(a, 1, 2), C) for a in (q, k, v, g, beta))
    incl = jnp.tril(jnp.ones((C, C), bool))
    strict = jnp.tril(jnp.ones((C, C), bool), -1)

    def step(S, inp):
        qc, kc, vc, gc, bc = inp
        gcum = jnp.cumsum(gc, axis=-1)
        diff = gcum[..., :, None] - gcum[..., None, :]
        decay = jnp.where(incl, jnp.exp(jnp.where(incl, diff, 0.0)), 0.0)
        kb = kc * bc[..., None]
        A = jnp.where(strict, jnp.einsum('bhtd,bhsd->bhts', kb, kc) * decay, 0.0)
        rhs = jnp.concatenate([vc * bc[..., None], kb * jnp.exp(gcum)[..., None]], axis=-1)
        sol = lax.linalg.triangular_solve(A, rhs, left_side=True, lower=True,
                                          unit_diagonal=True)
        u, w = sol[..., :GDN_DV], sol[..., GDN_DV:]
        v_new = u - jnp.einsum('bhtd,bhde->bhte', w, S)
        att = jnp.einsum('bhtd,bhsd->bhts', qc, kc) * decay
        o = (jnp.einsum('bhtd,bhde->bhte', qc * jnp.exp(gcum)[..., None], S)
             + jnp.einsum('bhts,bhse->bhte', att, v_new))
        gl = gcum[..., -1]
        S = (S * jnp.exp(gl)[..., None, None]
             + jnp.einsum('bhsd,bhse->bhde', kc * jnp.exp(gl[..., None] - gcum)[..., None], v_new))
        return S, o

    S, o = lax.scan(step, S0, xs)
    return _from_chunks(o, T), S


def _rglru(xc, w_a, b_a, w_x, b_x, lam, h0):
    B, T, W = xc.shape
    xb = xc.reshape(B, T, LRU_BLOCKS, LRU_BLOCK)
    r = jax.nn.sigmoid(jnp.einsum('btni,nij->btnj', xb, w_a).reshape(B, T, W) + b_a)
    i = jax.nn.sigmoid(jnp.einsum('btni,nij->btnj', xb, w_x).reshape(B, T, W) + b_x)
    log_a = -LRU_C * r * jax.nn.softplus(-lam)
    a = jnp.exp(log_a)
    b = jnp.sqrt(-jnp.expm1(2.0 * log_a)) * (i * xc)
    b = b.at[:, 0].add(a[:, 0] * h0)

    def combine(lhs, rhs):
        a1, b1 = lhs
        a2, b2 = rhs
        return a1 * a2, a2 * b1 + b2

    _, h = lax.associative_scan(combine, (a, b), axis=1)
    return h, h[:, -1]


def _layer(x, c, st, p, l, lb):
    s_hg, s_gdn, b_qkv, s_lru, b_lru, b_ffn = st
    B, T, _ = x.shape
    f32 = jnp.float32
    mod = jax.nn.silu(c) @ p['w_ada'][l] + p['b_ada'][l]
    sh1, sc1, g1, sh2, sc2, g2 = [m[:, None, :] for m in jnp.split(mod, 6, axis=-1)]

    h = _rmsnorm(x, p['norm_mix'][l]) * (1 + sc1) + sh1
    u = h @ p['w_in'][l]
    split_idx = np.cumsum(IN_SPLITS)[:-1].tolist()
    hq, hf, hi, hz, qkv, gz, ga, gb, lx, lz = jnp.split(u, split_idx, axis=-1)

    def heads(a, H):
        return a.reshape(B, T, H, -1)

    zf = hf.astype(f32)
    fg = lb + (1 - lb) * jax.nn.sigmoid(zf)
    logf = jnp.log(fg)
    kf = (1 - lb) * jax.nn.sigmoid(-zf)
    o_hg, s_hg = _hgrn2(heads(jax.nn.silu(hq.astype(f32)), HG_HEADS), heads(kf, HG_HEADS),
                        heads(hi.astype(f32), HG_HEADS), heads(logf, HG_HEADS),
                        s_hg.astype(f32))
    o_hg = _rmsnorm(o_hg, p['hg_norm'][l]) * jax.nn.silu(heads(hz.astype(f32), HG_HEADS))

    qkv_c, b_qkv = _causal_dwconv(qkv, b_qkv, p['gdn_conv_w'][l])
    qkv_c = jax.nn.silu(qkv_c.astype(f32))
    q, k, v = jnp.split(qkv_c, [GDN_HEADS * GDN_DK, 2 * GDN_HEADS * GDN_DK], axis=-1)
    q = _l2norm(heads(q, GDN_HEADS)) * (GDN_DK ** -0.5)
    k = _l2norm(heads(k, GDN_HEADS))
    v = heads(v, GDN_HEADS)
    g = -jnp.exp(p['gdn_a_log'][l].astype(f32)) * jax.nn.softplus(ga.astype(f32) + p['gdn_dt_bias'][l])
    beta = jax.nn.sigmoid(gb.astype(f32))
    o_gdn, s_gdn = _gated_delta(q, k, v, g, beta, s_gdn.astype(f32))
    o_gdn = _rmsnorm(o_gdn, p['gdn_norm'][l]) * jax.nn.silu(heads(gz.astype(f32), GDN_HEADS))

    xc, b_lru = _causal_dwconv(lx, b_lru, p['lru_conv_w'][l])
    xc = xc.astype(f32) + p['lru_conv_b'][l]
    hl, s_lru = _rglru(xc, p['lru_w_a'][l], p['lru_b_a'][l], p['lru_w_x'][l], p['lru_b_x'][l],
                       p['lru_lambda'][l], s_lru.astype(f32))
    o_lru = hl * jax.nn.gelu(lz.astype(f32))

    o = jnp.concatenate([o_hg.reshape(B, T, HG_WIDTH), o_gdn.reshape(B, T, GDN_WIDTH), o_lru],
                        axis=-1).astype(x.dtype)
    x = x + g1 * (o @ p['w_out'][l])

    h = _rmsnorm(x, p['norm_ffn'][l]) * (1 + sc2) + sh2
    gate, b_ffn = _causal_dwconv(h @ p['ffn_w_gate'][l], b_ffn, p['ffn_conv_w'][l])
    gate = gate + p['ffn_conv_b'][l]
    x = x + g2 * ((jax.nn.silu(gate) * (h @ p['ffn_w_up'][l])) @ p['ffn_w_down'][l])
    return x, (s_hg, s_gdn, b_qkv, s_lru, b_lru, b_ffn)


def _trunk(x, c, states, p, lbs, final_norm):
    collected = [[] for _ in range(6)]
    for l in range(DEPTH):
        st = tuple(s[l] for s in states)
        x, new = _layer(x, c, st, p, l, lbs[l])
        for lst, s in zip(collected, new):
            lst.append(s)
    y = _rmsnorm(x, final_norm)
    return y, tuple(jnp.stack(lst, axis=0) for lst in collected)


def setup_inputs(seed: int = 0) -> dict:
    key = jax.random.key(seed)
    ks = iter(jax.random.split(key, 48))
    f32 = jnp.float32

    def nrm(shape, s):
        return jax.random.normal(next(ks), shape, f32) * s

    x_prompt = nrm((BATCH, SEQ, D_MODEL), 1.0)
    x_sample = nrm((DEC_BATCH, DEC_SEQ, D_MODEL), 1.0)
    state_hgrn = nrm((DEPTH, DEC_BATCH, HG_HEADS, HG_DK, HG_DV), 0.5)
    state_gdn = nrm((DEPTH, DEC_BATCH, GDN_HEADS, GDN_DK, GDN_DV), 0.1)
    state_gdn_conv = nrm((DEPTH, DEC_BATCH, CONV_QKV - 1, GDN_QKV), 1.0)
    state_lru = nrm((DEPTH, DEC_BATCH, LRU_WIDTH), 0.5)
    state_lru_conv = nrm((DEPTH, DEC_BATCH, CONV_LRU - 1, LRU_WIDTH), 1.0)
    state_ffn_conv = nrm((DEPTH, DEC_BATCH, CONV_FFN - 1, D_FF), 1.0)
    c_prompt = nrm((BATCH, D_MODEL), 1.0)
    c_sample = nrm((DEC_BATCH, D_MODEL), 1.0)
    norm_mix = 1.0 + nrm((DEPTH, D_MODEL), 0.01)
    norm_ffn = 1.0 + nrm((DEPTH, D_MODEL), 0.01)
    w_ada = nrm((DEPTH, D_MODEL, 6 * D_MODEL), 0.5 * D_MODEL ** -0.5)
    b_ada = nrm((DEPTH, 6 * D_MODEL), 0.1)
    w_in = nrm((DEPTH, D_MODEL, D_IN), D_MODEL ** -0.5)
    hg_lower_bound = nrm((DEPTH, HG_FDIM), 1.0)
    hg_norm = 1.0 + nrm((DEPTH, HG_DV), 0.01)
    gdn_conv_w = nrm((DEPTH, CONV_QKV, GDN_QKV), CONV_QKV ** -0.5)
    gdn_a_log = jnp.log(jax.random.uniform(next(ks), (DEPTH, GDN_HEADS), f32, 1.0, 16.0))
    dt = jnp.exp(jax.random.uniform(next(ks), (DEPTH, GDN_HEADS), f32,
                                    float(np.log(1e-3)), float(np.log(1e-1))))
    gdn_dt_bias = dt + jnp.log(-jnp.expm1(-dt))
    gdn_norm = 1.0 + nrm((DEPTH, GDN_DV), 0.01)
    lru_conv_w = nrm((DEPTH, CONV_LRU, LRU_WIDTH), CONV_LRU ** -0.5)
    lru_conv_b = nrm((DEPTH, LRU_WIDTH), 0.02)
    lru_w_a = nrm((DEPTH, LRU_BLOCKS, LRU_BLOCK, LRU_BLOCK), LRU_BLOCK ** -0.5)
    lru_b_a = nrm((DEPTH, LRU_WIDTH), 0.02)
    lru_w_x = nrm((DEPTH, LRU_BLOCKS, LRU_BLOCK, LRU_BLOCK), LRU_BLOCK ** -0.5)
    lru_b_x = nrm((DEPTH, LRU_WIDTH), 0.02)
    a0 = jax.random.uniform(next(ks), (DEPTH, LRU_WIDTH), f32, 0.9, 0.999) ** (1.0 / LRU_C)
    lru_lambda = jnp.log(a0) - jnp.log1p(-a0)
    w_out = nrm((DEPTH, D_MIX, D_MODEL), D_MIX ** -0.5)
    ffn_w_gate = nrm((DEPTH, D_MODEL, D_FF), D_MODEL ** -0.5)
    ffn_w_up = nrm((DEPTH, D_MODEL, D_FF), D_MODEL ** -0.5)
    ffn_conv_w = nrm((DEPTH, CONV_FFN, D_FF), CONV_FFN ** -0.5)
    ffn_conv_b = nrm((DEPTH, D_FF), 0.02)
    ffn_w_down = nrm((DEPTH, D_FF, D_MODEL), D_FF ** -0.5)
    final_norm = 1.0 + nrm((D_MODEL,), 0.01)
    return {
        "x_prompt": x_prompt, "x_sample": x_sample,
        "state_hgrn": state_hgrn, "state_gdn": state_gdn, "state_gdn_conv": state_gdn_conv,
        "state_lru": state_lru, "state_lru_conv": state_lru_conv, "state_ffn_conv": state_ffn_conv,
        "c_prompt": c_prompt, "c_sample": c_sample,
        "norm_mix": norm_mix, "norm_ffn": norm_ffn, "w_ada": w_ada, "b_ada": b_ada,
        "w_in": w_in, "hg_lower_bound": hg_lower_bound, "hg_norm": hg_norm,
        "gdn_conv_w": gdn_conv_w, "gdn_a_log": gdn_a_log, "gdn_dt_bias": gdn_dt_bias,
        "gdn_norm": gdn_norm, "lru_conv_w": lru_conv_w, "lru_conv_b": lru_conv_b,
        "lru_w_a": lru_w_a, "lru_b_a": lru_b_a, "lru_w_x": lru_w_x, "lru_b_x": lru_b_x,
        "lru_lambda": lru_lambda, "w_out": w_out, "ffn_w_gate": ffn_w_gate,
        "ffn_w_up": ffn_w_up, "ffn_conv_w": ffn_conv_w, "ffn_conv_b": ffn_conv_b,
        "ffn_w_down": ffn_w_down, "final_norm": final_norm,
    }


def reference(x_prompt, x_sample, state_hgrn, state_gdn, state_gdn_conv, state_lru,
              state_lru_conv, state_ffn_conv, c_prompt, c_sample, norm_mix, norm_ffn,
              w_ada, b_ada, w_in, hg_lower_bound, hg_norm, gdn_conv_w, gdn_a_log,
              gdn_dt_bias, gdn_norm, lru_conv_w, lru_conv_b, lru_w_a, lru_b_a, lru_w_x,
              lru_b_x, lru_lambda, w_out, ffn_w_gate, ffn_w_up, ffn_conv_w, ffn_conv_b,
              ffn_w_down, final_norm):
    f32 = jnp.float32
    p = dict(norm_mix=norm_mix, norm_ffn=norm_ffn, w_ada=w_ada, b_ada=b_ada, w_in=w_in,
             hg_norm=hg_norm, gdn_conv_w=gdn_conv_w, gdn_a_log=gdn_a_log,
             gdn_dt_bias=gdn_dt_bias, gdn_norm=gdn_norm, lru_conv_w=lru_conv_w,
             lru_conv_b=lru_conv_b, lru_w_a=lru_w_a, lru_b_a=lru_b_a, lru_w_x=lru_w_x,
             lru_b_x=lru_b_x, lru_lambda=lru_lambda, w_out=w_out, ffn_w_gate=ffn_w_gate,
             ffn_w_up=ffn_w_up, ffn_conv_w=ffn_conv_w, ffn_conv_b=ffn_conv_b,
             ffn_w_down=ffn_w_down)
    sm = jax.nn.softmax(hg_lower_bound.astype(f32), axis=0)
    lbs = jnp.cumsum(sm, axis=0) - sm[0]

    Bp = x_prompt.shape[0]
    zero_states = (
        jnp.zeros((DEPTH, Bp, HG_HEADS, HG_DK, HG_DV), f32),
        jnp.zeros((DEPTH, Bp, GDN_HEADS, GDN_DK, GDN_DV), f32),
        jnp.zeros((DEPTH, Bp, CONV_QKV - 1, GDN_QKV), x_prompt.dtype),
        jnp.zeros((DEPTH, Bp, LRU_WIDTH), f32),
        jnp.zeros((DEPTH, Bp, CONV_LRU - 1, LRU_WIDTH), x_prompt.dtype),
        jnp.zeros((DEPTH, Bp, CONV_FFN - 1, D_FF), x_prompt.dtype),
    )
    y_prompt, (hg_p, gdn_p, gconv_p, lru_p, lconv_p, fconv_p) = _trunk(
        x_prompt, c_prompt, zero_states, p, lbs, final_norm)
    sample_states = (state_hgrn, state_gdn, state_gdn_conv, state_lru, state_lru_conv,
                     state_ffn_conv)
    y_sample, (hg_s, gdn_s, gconv_s, lru_s, lconv_s, fconv_s) = _trunk(
        x_sample, c_sample, sample_states, p, lbs, final_norm)
    return (y_prompt, y_sample, hg_p, gdn_p, gconv_p, lru_p, lconv_p, fconv_p,
            hg_s, gdn_s, gconv_s, lru_s, lconv_s, fconv_s)
```

```python
import numpy as np
import concourse.bass as bass
import concourse.mybir as mybir
from concourse.bass_utils import run_bass_kernel_spmd

F32 = mybir.dt.float32
BF16 = mybir.dt.bfloat16
AF = mybir.ActivationFunctionType
ALU = mybir.AluOpType

NCORES = 8
D = 1024
TP = 2048
NSQ = 16
TSQ = 4
NTS = NSQ * TSQ
NT = TP + NTS
DFF = 2816
NFC = 22
DIN = 3592
EPS = 1e-6
SLOT = 12288


class Op:
    __slots__ = ("eng", "fn", "deps", "dma", "signal", "sem", "val", "guard")

    def __init__(self, eng, fn, dma):
        self.eng = eng
        self.fn = fn
        self.deps = []
        self.dma = dma
        self.signal = dma
        self.sem = None
        self.val = 0
        self.guard = None


def _box(ap):
    name = ap.tensor.name
    pat = ap.ap
    off = ap.offset
    sz = mybir.dt.size(ap.dtype)
    space = str(ap.space)
    if space == "PSUM":
        return (name, 0, 128, 0, 2048, True)
    if space in ("SB", "PSUM"):
        pstep, pcnt = pat[0]
        if pstep == 0:
            p0 = 0
            f0 = off
        else:
            p0 = off // pstep
            f0 = off - p0 * pstep
        ext = 0
        foot = 1
        for st, cn in pat[1:]:
            ext += abs(st) * (cn - 1)
            if st != 0:
                foot *= cn
        return (name, p0, p0 + pcnt, f0 * sz, (f0 + ext + 1) * sz, foot == ext + 1)
    ext = 0
    foot = 1
    for st, cn in pat:
        ext += abs(st) * (cn - 1)
        if st != 0:
            foot *= cn
    return (name, 0, 1, off * sz, (off + ext + 1) * sz, foot == ext + 1)


def _ov(a, b):
    return a[1] < b[2] and b[1] < a[2] and a[3] < b[4] and b[3] < a[4]


def _cov(b, r):
    return b[1] <= r[1] and r[2] <= b[2] and b[3] <= r[3] and r[4] <= b[4]


class Prog:
    ENGS = ("pe", "act", "dve", "pool", "sp")

    def __init__(self, nc):
        self.nc = nc
        self.ops = {e: [] for e in self.ENGS}
        self.recs = {}
        self.all_dma = []

    def add(self, eng, fn, reads, writes, dma=False):
        op = Op(eng, fn, dma)
        rboxes = [_box(a) for a in reads]
        wboxes = [_box(a) for a in writes]
        raw = set()
        other = set()
        for b in rboxes:
            ws, rs = self.recs.setdefault(b[0], ([], []))
            for (wb, wop) in ws:
                if _ov(wb, b):
                    raw.add(wop)
        for b in wboxes:
            ws, rs = self.recs.setdefault(b[0], ([], []))
            for (wb, wop) in ws:
                if _ov(wb, b):
                    other.add(wop)
            for (rb, rop) in rs:
                if _ov(rb, b):
                    other.add(rop)
        for d in raw | other:
            if d is op:
                continue
            if (not d.dma) and (not dma) and d.eng == eng and eng == "pe":
                continue
            op.deps.append(d)
        for b in rboxes:
            ws, rs = self.recs[b[0]]
            if not dma:
                rs[:] = [(rb, rop) for (rb, rop) in rs
                         if not (rop.eng == eng and not rop.dma and _cov(b, rb))]
            rs.append((b, op))
        for b in wboxes:
            ws, rs = self.recs[b[0]]
            if b[5]:
                ws[:] = [(wb, wop) for (wb, wop) in ws if not _cov(b, wb)]
                rs[:] = [(rb, rop) for (rb, rop) in rs if not _cov(b, rb)]
            ws.append((b, op))
        self.ops[eng].append(op)
        if dma:
            self.all_dma.append(op)
        return op

    def mm(self, out, lhsT, rhs, start=True, stop=True):
        return self.add("pe", lambda e: e.matmul(out, lhsT, rhs, start=start, stop=stop),
                        [lhsT, rhs], [out])

    def tr(self, out, in_, ident):
        return self.add("pe", lambda e: e.transpose(out, in_, ident), [in_, ident], [out])

    def act(self, out, in_, func, bias=None, scale=None):
        kw = {}
        rd = [in_]
        if bias is not None:
            kw["bias"] = bias
            if not isinstance(bias, (int, float)):
                rd.append(bias)
        if scale is not None:
            kw["scale"] = scale
            if not isinstance(scale, (int, float)):
                rd.append(scale)
        return self.add("act", lambda e: e.activation(out, in_, func, **kw), rd, [out])

    def tt(self, out, in0, in1, op, eng="dve"):
        return self.add(eng, lambda e: e.tensor_tensor(out, in0, in1, op), [in0, in1], [out])

    def ts(self, out, in0, s1, op0, s2=None, op1=None, eng="dve"):
        rd = [in0]
        if not isinstance(s1, (int, float)):
            rd.append(s1)
        if s2 is not None and not isinstance(s2, (int, float)):
            rd.append(s2)
        if op1 is None:
            return self.add(eng, lambda e: e.tensor_scalar(out, in0, s1, None, op0), rd, [out])
        return self.add(eng, lambda e: e.tensor_scalar(out, in0, s1, s2, op0, op1), rd, [out])

    def stt(self, out, in0, scalar, in1, op0, op1):
        rd = [in0, in1]
        if not isinstance(scalar, (int, float)):
            rd.append(scalar)
        return self.add("dve", lambda e: e.scalar_tensor_tensor(out, in0, scalar, in1, op0, op1),
                        rd, [out])

    def scan(self, out, d0, d1, init):
        rd = [d0, d1]
        if not isinstance(init, (int, float)):
            rd.append(init)
        return self.add("dve", lambda e: e.tensor_tensor_scan(out, d0, d1, init, ALU.mult, ALU.add),
                        rd, [out])

    def copy(self, out, in_, eng="dve"):
        if eng == "act":
            return self.add("act", lambda e: e.copy(out, in_), [in_], [out])
        return self.add(eng, lambda e: e.tensor_copy(out, in_), [in_], [out])

    def memset(self, out, val, eng="dve"):
        return self.add(eng, lambda e: e.memset(out, val), [], [out])

    def recip(self, out, in_):
        return self.add("dve", lambda e: e.reciprocal(out, in_), [in_], [out])

    def dma(self, out, in_, eng="sp"):
        return self.add(eng, lambda e: e.dma_start(out, in_), [in_], [out], dma=True)

    def emit(self, n_dma_sems=(("sp", 56), ("pool", 32))):
        nc = self.nc
        for e in self.ENGS:
            for op in self.ops[e]:
                for d in op.deps:
                    d.signal = True
        esem = {e: nc.alloc_semaphore(name="s_" + e) for e in ("pe", "act", "dve", "pool")}
        pools = {e: [nc.alloc_semaphore(name="d_%s_%d" % (e, i)) for i in range(n)]
                 for e, n in n_dma_sems}
        for e in self.ENGS:
            cnt = 0
            k = 0
            for op in self.ops[e]:
                if op.dma:
                    pool = pools[e]
                    i = k % len(pool)
                    op.sem = pool[i]
                    op.val = 16 * (k // len(pool) + 1)
                    if k >= len(pool):
                        op.guard = (pool[i], op.val - 16)
                    k += 1
                elif op.signal:
                    cnt += 1
                    op.sem = esem[e]
                    op.val = cnt
        final = {}
        for op in self.all_dma:
            key = id(op.sem)
            if final.get(key, (None, 0))[1] < op.val:
                final[key] = (op.sem, op.val)
        prog = self

        def emit_engine(eng_obj, e):
            waited = {}
            for op in prog.ops[e]:
                need = {}
                for d in op.deps:
                    key = id(d.sem)
                    if need.get(key, (None, 0))[1] < d.val:
                        need[key] = (d.sem, d.val)
                if op.guard is not None:
                    key = id(op.guard[0])
                    if need.get(key, (None, 0))[1] < op.guard[1]:
                        need[key] = op.guard
                for key, (sem, val) in need.items():
                    if waited.get(key, 0) < val:
                        eng_obj.wait_ge(sem, val)
                        waited[key] = val
                ins = op.fn(eng_obj)
                if op.dma:
                    ins.then_inc(op.sem, 16)
                elif op.signal:
                    ins.then_inc(op.sem, 1)
            if e == "sp":
                for key, (sem, val) in final.items():
                    if waited.get(key, 0) < val:
                        eng_obj.wait_ge(sem, val)

        with nc.Block() as block:
            @block.tensor
            def _(eng):
                emit_engine(eng, "pe")

            @block.scalar
            def _(eng):
                emit_engine(eng, "act")

            @block.vector
            def _(eng):
                emit_engine(eng, "dve")

            @block.gpsimd
            def _(eng):
                emit_engine(eng, "pool")

            @block.sync
            def _(eng):
                emit_engine(eng, "sp")


class Arena:
    def __init__(self, nc, name, nbytes, tensor=None, base=0):
        self.t = tensor if tensor is not None else nc.alloc_sbuf_tensor(name, [128, nbytes // 4], F32)
        self.base = base
        self.nbytes = nbytes
        self.off = 0
        self.peak = 0

    def reset(self, off=0):
        self.off = off

    def alloc(self, shape, dtype=F32):
        sz = mybir.dt.size(dtype)
        n = 1
        for s in shape[1:]:
            n *= s
        nb = (n * sz + 31) // 32 * 32
        assert self.off + nb <= self.nbytes, ("arena overflow", self.off, nb, self.nbytes)
        base = self.t if dtype == self.t.dtype else self.t.bitcast(dtype)
        e0 = (self.base + self.off) // sz
        v = base[0:shape[0], e0:e0 + n]
        self.off += nb
        self.peak = max(self.peak, self.off)
        if len(shape) > 2:
            names = "abcdefg"[: len(shape) - 1]
            pat = "p (%s) -> p %s" % (" ".join(names), " ".join(names))
            v = v.rearrange(pat, **{names[i]: shape[i + 1] for i in range(len(shape) - 1)})
        return v


def bc(ap, shape):
    return ap.to_broadcast(list(shape))


class _Stop(Exception):
    pass


def build_program(stop=None):
    nc = bass.Bass("TRN2", target_bir_lowering=False)
    P = Prog(nc)
    try:
        _body(nc, P, stop)
    except _Stop:
        pass
    P.emit()
    return nc


def _body(nc, P, stop):
    def chk(tag):
        if stop == tag:
            raise _Stop()

    def din(name, shape):
        return nc.dram_tensor(name, list(shape), F32, kind="ExternalInput").ap()

    def dout(name, shape):
        return nc.dram_tensor(name, list(shape), F32, kind="ExternalOutput").ap()

    xT_d = din("xT", [D, NT])
    cT_d = din("cT", [D, 17])
    st_hg_d = din("st_hg", [2, 128, NSQ, 2, 64])
    st_gdn_d = din("st_gdn", [2, NSQ, 4, 128, 128])
    st_gconv_d = din("st_gconv", [2, 128, 12, NSQ, 3])
    st_lru_d = din("st_lru", [2, 128, 2, NSQ])
    st_lconv_d = din("st_lconv", [2, 128, 2, NSQ, 3])
    st_fconv_d = din("st_fconv", [2, 128, NFC, NSQ, 2])
    w_ada_d = din("w_ada", [2, D, 6 * D])
    b_adaT_d = din("b_adaT", [2, 128, 48])
    w_in_d = din("w_in", [2, D, DIN])
    w_out_d = din("w_out", [2, D, D])
    w_gate_d = din("w_gate", [2, D, DFF])
    w_up_d = din("w_up", [2, D, DFF])
    w_down_d = din("w_down", [2, DFF, D])
    nmixT_d = din("nmixT", [2, 128, 8])
    nffnT_d = din("nffnT", [2, 128, 8])
    fnormT_d = din("fnormT", [128, 8])
    hlbT_d = din("hlbT", [2, 128, 2])
    hgnT_d = din("hgnT", [64, 2])
    gcwT_d = din("gcwT", [2, 128, 12, 4])
    galog_d = din("galog", [2, 4])
    gdtb_d = din("gdtb", [2, 4])
    gnT_d = din("gnT", [128, 2])
    lcwT_d = din("lcwT", [2, 128, 2, 4])
    lcbT_d = din("lcbT", [2, 128, 2])
    lwa_d = din("lwa", [2, 4, 64, 64])
    lwx_d = din("lwx", [2, 4, 64, 64])
    lbaT_d = din("lbaT", [2, 128, 2])
    lbxT_d = din("lbxT", [2, 128, 2])
    llamT_d = din("llamT", [2, 128, 2])
    fcwT_d = din("fcwT", [2, 128, NFC, 3])
    fcbT_d = din("fcbT", [2, 128, NFC])

    yT_o = dout("yT", [D, NT])
    hg_p_o = dout("o_hg_p", [2, 128, 2, 64])
    gdn_p_o = dout("o_gdn_p", [2, 4, 128, 128])
    gconv_p_o = dout("o_gconv_p", [2, 128, 12, 3])
    lru_p_o = dout("o_lru_p", [2, 128, 2])
    lconv_p_o = dout("o_lconv_p", [2, 128, 2, 3])
    fconv_p_o = dout("o_fconv_p", [2, 128, NFC, 2])
    hg_s_o = dout("o_hg_s", [2, 128, NSQ, 2, 64])
    gdn_s_o = dout("o_gdn_s", [2, NSQ, 4, 128, 128])
    gconv_s_o = dout("o_gconv_s", [2, 128, 12, NSQ, 3])
    lru_s_o = dout("o_lru_s", [2, 128, 2, NSQ])
    lconv_s_o = dout("o_lconv_s", [2, 128, 2, NSQ, 3])
    fconv_s_o = dout("o_fconv_s", [2, 128, NFC, NSQ, 2])

    def sb(name, shape, dt=F32):
        return nc.alloc_sbuf_tensor(name, list(shape), dt)

    x = sb("x_res", [128, 8, NT])
    hb = sb("h_bf", [128, 8, NT], BF16)
    warena = sb("warena", [128, 2 * SLOT], BF16)
    ident = sb("ident", [128, 128])
    ones_f = sb("ones_f", [128, 128])
    nones_f = sb("nones_f", [64, 64])
    ones_b = sb("ones_b", [128, 128], BF16)
    tri = sb("tri", [64, 64])
    nega = sb("nega", [64, 64])
    negt = sb("negt", [64, 64])
    rm32 = sb("rm32", [128, 256])
    rm4 = sb("rm4", [128, 64])
    mods = sb("mods", [128, 2, 48, 17])
    b_adaT = sb("b_adaT_s", [128, 2, 48])
    nmixT = sb("nmixT_s", [128, 2, 8])
    nffnT = sb("nffnT_s", [128, 2, 8])
    fnormT = sb("fnormT_s", [128, 8])
    coefA = sb("coefA", [128, 8, 17])
    hlbT = sb("hlbT_s", [128, 2, 2])
    lbT = sb("lbT", [128, 2, 2])
    omlT = sb("omlT", [128, 2, 2])
    nomlT = sb("nomlT", [128, 2, 2])
    hgnT = sb("hgnT_s", [64, 2])
    gcwT = sb("gcwT_s", [128, 2, 12, 4])
    negA = sb("negA", [64, 2, 4])
    dtb = sb("dtb", [64, 2, 4])
    gnT = sb("gnT_s", [128, 2])
    lcwT = sb("lcwT_s", [128, 2, 2, 4])
    lcbT = sb("lcbT_s", [128, 2, 2])
    lbaT = sb("lbaT_s", [128, 2, 2])
    lbxT = sb("lbxT_s", [128, 2, 2])
    nl8 = sb("nl8", [128, 2, 2])
    wabd = sb("wabd", [128, 2, 2, 128])
    wxbd = sb("wxbd", [128, 2, 2, 128])
    fcwT = sb("fcwT_s", [128, 2, NFC, 3])
    fcbT = sb("fcbT_s", [128, 2, NFC])
    S_hg = sb("S_hg", [128, 2, 64])
    S_hgB = sb("S_hgB", [128, 2, 64])
    hg_cur = [0]
    S_gd = sb("S_gd", [128, 2, 128])
    h_lru = sb("h_lru", [128, 2])
    fcarry = sb("fcarry", [128, NFC, 2])

    W = Arena(nc, "work", 36 * 1024)
    W2 = Arena(nc, "w2", SLOT * 2, tensor=warena, base=SLOT * 2)
    pb = [nc.alloc_psum_tensor("pb%d" % i, [128, 512], F32) for i in range(8)]

    slot_ctr = [0]

    def next_slot(force=None):
        if force is not None:
            slot_ctr[0] = force + 1
            return warena[:, force * SLOT:(force + 1) * SLOT]
        s = slot_ctr[0] % 2
        slot_ctr[0] += 1
        return warena[:, s * SLOT:(s + 1) * SLOT]

    def seq_tiles(TT, sgrp=NSQ):
        tl = [dict(t0=t0, TT=TT, nseq=1, T=TT, samp=False, first=(t0 == 0), last=(t0 + TT == TP), s0=0)
              for t0 in range(0, TP, TT)]
        for s0 in range(0, NSQ, sgrp):
            tl.append(dict(t0=TP + s0 * TSQ, TT=sgrp * TSQ, nseq=sgrp, T=TSQ, samp=True, first=False,
                           last=False, s0=s0))
        return tl

    P.memset(ident[:], 0.0)
    P.add("pool", lambda e: e.affine_select(out=ident[:], in_=ident[:], pattern=[[-1, 128]],
                                            compare_op=ALU.not_equal, fill=1.0, base=0,
                                            channel_multiplier=1), [ident[:]], [ident[:]])
    P.memset(ones_f[:], 1.0)
    P.memset(nones_f[:], -1.0)
    P.memset(ones_b[:], 1.0)
    P.memset(tri[:], 1.0)
    P.add("pool", lambda e: e.affine_select(out=tri[:], in_=tri[:], pattern=[[1, 64]],
                                            compare_op=ALU.is_ge, fill=0.0, base=0,
                                            channel_multiplier=-1), [tri[:]], [tri[:]])
    P.memset(nega[:], 0.0)
    P.add("pool", lambda e: e.affine_select(out=nega[:], in_=nega[:], pattern=[[-1, 64]],
                                            compare_op=ALU.is_gt, fill=-30000.0, base=0,
                                            channel_multiplier=1), [nega[:]], [nega[:]])
    P.memset(negt[:], 0.0)
    P.add("pool", lambda e: e.affine_select(out=negt[:], in_=negt[:], pattern=[[1, 64]],
                                            compare_op=ALU.is_ge, fill=-30000.0, base=0,
                                            channel_multiplier=-1), [negt[:]], [negt[:]])
    P.memset(rm32[:], 1.0)
    P.memset(rm32[:].rearrange("p (c t) -> p c t", t=32)[:, :, 0:1], 0.0)
    P.memset(rm4[:], 1.0)
    P.memset(rm4[:].rearrange("p (c t) -> p c t", t=4)[:, :, 0:1], 0.0)

    P.dma(x[:], xT_d.rearrange("(k p) t -> p k t", p=128))
    P.dma(b_adaT[:], b_adaT_d.rearrange("l p c -> p l c"))
    P.dma(nmixT[:], nmixT_d.rearrange("l p c -> p l c"))
    P.dma(nffnT[:], nffnT_d.rearrange("l p c -> p l c"))
    P.dma(fnormT[:], fnormT_d)
    P.dma(hlbT[:], hlbT_d.rearrange("l p c -> p l c"))
    P.dma(hgnT[:], hgnT_d)
    P.dma(gcwT[:], gcwT_d.rearrange("l p c j -> p l c j"))
    for l in range(2):
        P.dma(negA[:, l, :], galog_d[l:l + 1, :].partition_broadcast(64))
        P.dma(dtb[:, l, :], gdtb_d[l:l + 1, :].partition_broadcast(64))
    P.dma(gnT[:], gnT_d)
    P.dma(lcwT[:], lcwT_d.rearrange("l p c j -> p l c j"))
    P.dma(lcbT[:], lcbT_d.rearrange("l p c -> p l c"))
    P.dma(lbaT[:], lbaT_d.rearrange("l p c -> p l c"))
    P.dma(lbxT[:], lbxT_d.rearrange("l p c -> p l c"))
    P.dma(nl8[:], llamT_d.rearrange("l p c -> p l c"))
    P.dma(fcwT[:], fcwT_d.rearrange("l p c j -> p l c j"))
    P.dma(fcbT[:], fcbT_d.rearrange("l p c -> p l c"))
    P.memset(wabd[:], 0.0)
    P.memset(wxbd[:], 0.0)
    for l in range(2):
        for n in range(4):
            ch, j = n // 2, n % 2
            P.dma(wabd[64 * j:64 * j + 64, l, ch, 64 * j:64 * j + 64], lwa_d[l, n])
            P.dma(wxbd[64 * j:64 * j + 64, l, ch, 64 * j:64 * j + 64], lwx_d[l, n])
    P.act(negA[:], negA[:], AF.Exp)
    P.ts(negA[:], negA[:], -1.0, ALU.mult)
    P.act(nl8[:], nl8[:], AF.Exp, scale=-1.0)
    P.act(nl8[:], nl8[:], AF.Ln, bias=1.0)
    P.ts(nl8[:], nl8[:], -8.0, ALU.mult)
    P.memset(lbT[:, 0, :], 0.0)
    P.tt(lbT[:, 1, :], hlbT[:, 1, :], hlbT[:, 0, :], ALU.subtract)
    P.act(lbT[:, 1, :], lbT[:, 1, :], AF.Sigmoid)
    P.ts(omlT[:], lbT[:], -1.0, ALU.mult, 1.0, ALU.add)
    P.ts(nomlT[:], omlT[:], -1.0, ALU.mult)

    chk('c')
    W.reset()
    cT = W.alloc([128, 8, 17])
    scT = W.alloc([128, 8, 17], BF16)
    P.dma(cT, cT_d.rearrange("(k p) s -> p k s", p=128))
    P.act(scT, cT, AF.Silu)
    gi = 0
    for l in range(2):
        for g in range(12):
            sl = next_slot()
            sub = sl[:, 0:4096].rearrange("p (k n) -> p k n", k=8)
            P.dma(sub, w_ada_d[l].rearrange("(k p) n -> p k n", p=128)[:, :, 512 * g:512 * g + 512],
                  eng="pool")
            bank = pb[gi % 2]
            gi += 1
            for mc in range(4):
                for k in range(8):
                    P.mm(bank[:, mc * 17:mc * 17 + 17], sub[:, k, mc * 128:mc * 128 + 128],
                         scT[:, k, :], start=(k == 0), stop=(k == 7))
            for mc in range(4):
                chunk = g * 4 + mc
                plus = 1.0 if (8 <= chunk < 16 or 32 <= chunk < 40) else 0.0
                P.ts(mods[:, l, chunk, :], bank[:, mc * 17:mc * 17 + 17],
                     b_adaT[:, l, chunk:chunk + 1], ALU.add, plus, ALU.add)

    chk('m')
    def modnorm(l, normT, sc_off, sh_off):
        P.tt(coefA[:], mods[:, l, sc_off:sc_off + 8, :],
             bc(normT[:, l, :].unsqueeze(2), [128, 8, 17]), ALU.mult)
        for tl in seq_tiles(256):
            t0, TT = tl["t0"], tl["TT"]
            W.reset()
            sq = W.alloc([128, 8, TT], BF16)
            tmp = W.alloc([128, 8, TT])
            rs = W.alloc([128, TT])
            P.act(sq, x[:, :, t0:t0 + TT], AF.Square)
            ps = pb[0][:, 0:TT]
            for k in range(8):
                P.mm(ps, ones_b[:], sq[:, k, :], start=(k == 0), stop=(k == 7))
            P.act(rs, ps, AF.Sqrt, bias=EPS, scale=1.0 / D)
            P.recip(rs, rs)
            P.tt(tmp, x[:, :, t0:t0 + TT], bc(rs.unsqueeze(1), [128, 8, TT]), ALU.mult)
            if not tl["samp"]:
                for k in range(8):
                    if k % 2 == 1:
                        P.act(hb[:, k, t0:t0 + TT], tmp[:, k, :], AF.Identity,
                              bias=mods[:, l, sh_off + k, 0:1], scale=coefA[:, k, 0:1])
                    else:
                        P.ts(hb[:, k, t0:t0 + TT], tmp[:, k, :], coefA[:, k, 0:1], ALU.mult,
                             mods[:, l, sh_off + k, 0:1], ALU.add)
            else:
                tv = tmp.rearrange("p k (s t) -> p k s t", t=TSQ)
                P.tt(tv, tv, bc(coefA[:, :, 1:17].unsqueeze(3), [128, 8, NSQ, TSQ]), ALU.mult)
                P.tt(hb[:, :, t0:t0 + TT].rearrange("p k (s t) -> p k s t", t=TSQ), tv,
                     bc(mods[:, l, sh_off:sh_off + 8, 1:17].unsqueeze(3), [128, 8, NSQ, TSQ]),
                     ALU.add)

    def x_update(l, g_off, m, ps, tl):
        t0, TT = tl["t0"], tl["TT"]
        xs = x[:, m, t0:t0 + TT]
        if not tl["samp"]:
            if X_tmp[0] is not None:
                tmp = X_tmp[0][X_tmp[1] % 2][:, 0:TT]
                X_tmp[1] += 1
                P.act(tmp, ps, AF.Identity, scale=mods[:, l, g_off + m, 0:1])
                P.tt(xs, xs, tmp, ALU.add, eng="pool")
            else:
                P.stt(xs, ps, mods[:, l, g_off + m, 0:1], xs, ALU.mult, ALU.add)
        else:
            s0, ns = tl["s0"], tl["nseq"]
            tmp = W_x[0][:, 0:TT]
            tv = tmp.rearrange("p (s t) -> p s t", t=TSQ)
            P.tt(tv, ps.rearrange("p (s t) -> p s t", t=TSQ),
                 bc(mods[:, l, g_off + m, 1 + s0:1 + s0 + ns].unsqueeze(2), [128, ns, TSQ]), ALU.mult)
            P.tt(xs, xs, tmp, ALU.add)

    W_x = [None]
    X_tmp = [None, 0]

    def proj(ps, wv, c0, ncol, t0, TT):
        for k in range(8):
            P.mm(ps, wv[:, k, c0:c0 + ncol], hb[:, k, t0:t0 + TT], start=(k == 0), stop=(k == 7))

    for l in range(2):
        modnorm(l, nmixT, 8, 0)

        chk('n%d' % l)
        sl = next_slot(0)
        win = sl[:, 0:8192].rearrange("p (k n) -> p k n", k=8)
        wout = sl[0:64, 8192:12288].rearrange("p (h n) -> p h n", h=4)
        P.dma(win, w_in_d[l].rearrange("(k p) n -> p k n", p=128)[:, :, 0:1024], eng="pool")
        P.dma(wout, w_out_d[l][0:256, :].rearrange("(h e) n -> e h n", e=64), eng="pool")
        for tl in seq_tiles(256, 8):
            t0, TT, samp = tl["t0"], tl["TT"], tl["samp"]
            s0, nsq = tl["s0"], tl["nseq"]
            C = 4 if samp else 32
            NC = TT // C
            W.reset()
            W2.reset()
            W_x[0] = W.alloc([128, NTS])
            late_off = W.off
            q = W.alloc([128, 2, TT])
            sg = W.alloc([128, 2, TT])
            kk = W.alloc([128, 2, TT])
            G = W.alloc([128, 2, TT])
            eG = W.alloc([128, 2, TT])
            vT = W.alloc([128, 2, TT])
            hzs = W.alloc([64, 4, TT])
            qgm = W.alloc([128, 2, 2, TT])
            kgl = W.alloc([128, 2, TT])
            eGl = W.alloc([128, 2, NC])
            AT = W.alloc([32, NC * 4, C])
            ohg = W.alloc([64, 4, TT])
            Ssm = W.alloc([128, nsq, 2, 64]) if samp else None
            kglm = W2.alloc([32, NC, 2, 2, 128])
            vtm = W2.alloc([32, NC, 2, 128])
            bi = 0
            for c in range(2):
                ps = pb[bi % 4][:, 0:TT]; bi += 1
                proj(ps, win, c * 128, 128, t0, TT)
                P.act(q[:, c, :], ps, AF.Silu)
            for c in range(2):
                ps = pb[bi % 4][:, 0:TT]; bi += 1
                proj(ps, win, 256 + c * 128, 128, t0, TT)
                P.act(sg[:, c, :], ps, AF.Sigmoid)
            for c in range(2):
                ps = pb[bi % 4][:, 0:TT]; bi += 1
                proj(ps, win, 512 + c * 128, 128, t0, TT)
                P.copy(vT[:, c, :], ps, eng="act")
            for h in range(4):
                ps = pb[bi % 4][0:64, 0:TT]; bi += 1
                proj(ps, win, 768 + h * 64, 64, t0, TT)
                P.act(hzs[:, h, :], ps, AF.Silu)
            for c in range(2):
                P.ts(kk[:, c, :], sg[:, c, :], nomlT[:, l, c:c + 1], ALU.mult,
                     omlT[:, l, c:c + 1], ALU.add)
                P.ts(sg[:, c, :], sg[:, c, :], omlT[:, l, c:c + 1], ALU.mult,
                     lbT[:, l, c:c + 1], ALU.add)
            P.act(sg, sg, AF.Ln)
            rm = rm4[:, 0:TT] if samp else rm32[:, 0:TT]
            for c in range(2):
                P.scan(G[:, c, :], rm, sg[:, c, :], 0.0)
            P.act(eG, G, AF.Exp)
            P.memset(qgm, 0.0)
            P.tt(qgm[0:64, :, 0, :], q[0:64], eG[0:64], ALU.mult)
            P.tt(qgm[64:128, :, 1, :], q[64:128], eG[64:128], ALU.mult)
            P.act(eG, G, AF.Exp, scale=-1.0)
            P.tt(kk, kk, eG, ALU.mult)
            Gv = G.rearrange("p a (c t) -> p a c t", t=C)
            P.act(eGl, Gv[:, :, :, C - 1], AF.Exp)
            P.tt(kgl.rearrange("p a (c t) -> p a c t", t=C), kk.rearrange("p a (c t) -> p a c t", t=C),
                 bc(eGl.unsqueeze(3), [128, 2, NC, C]), ALU.mult)
            psA = pb[4]
            psA2 = pb[5]
            per_bank = 512 // C
            for c in range(NC):
                for h in range(4):
                    pr = c * 4 + h
                    bank = psA if pr < per_bank else psA2
                    o = (pr % per_bank) * C
                    P.mm(bank[0:C, o:o + C], kk[:, h // 2, c * C:(c + 1) * C],
                         qgm[:, h // 2, h % 2, c * C:(c + 1) * C])
            npr = NC * 4
            n0 = min(npr, per_bank)
            P.tt(AT[0:C, 0:n0, :], psA[0:C, 0:n0 * C].rearrange("p (a t) -> p a t", t=C),
                 bc(tri[0:C, 0:C].unsqueeze(1), [C, n0, C]), ALU.mult)
            if npr > per_bank:
                n1 = npr - per_bank
                P.tt(AT[0:C, n0:npr, :], psA2[0:C, 0:n1 * C].rearrange("p (a t) -> p a t", t=C),
                     bc(tri[0:C, 0:C].unsqueeze(1), [C, n1, C]), ALU.mult)
            P.memset(kglm, 0.0)
            for c0 in range(0, NC, 2):
                bk = pb[6]
                bv = pb[7]
                for ci in range(2):
                    c = c0 + ci
                    for p2 in range(2):
                        o = (ci * 2 + p2) * 128
                        P.tr(bk[0:C, o:o + 128], kgl[:, p2, c * C:(c + 1) * C], ident[:])
                        P.tr(bv[0:C, o:o + 128], vT[:, p2, c * C:(c + 1) * C], ident[:])
                bkv = bk[0:C, :].rearrange("p (c a e) -> p c a e", c=2, a=2)
                P.copy(kglm[0:C, c0:c0 + 2, :, 0, 0:64], bkv[:, :, :, 0:64], eng="act")
                P.copy(kglm[0:C, c0:c0 + 2, :, 1, 64:128], bkv[:, :, :, 64:128], eng="act")
                P.copy(vtm[0:C, c0:c0 + 2, :, :], bv[0:C, :].rearrange("p (c a e) -> p c a e", c=2, a=2))
            if samp:
                P.dma(Ssm, st_hg_d[l][:, s0:s0 + nsq])
            elif tl["first"]:
                hg_cur[0] = 0
                P.memset(S_hg[:], 0.0)
            for c in range(NC):
                if samp:
                    S = Sn = Ssm[:, c]
                else:
                    S = (S_hg, S_hgB)[hg_cur[0]]
                    Sn = (S_hg, S_hgB)[1 - hg_cur[0]]
                    hg_cur[0] = 1 - hg_cur[0]
                po = pb[(c % 2)][0:64, 0:4 * C]
                for h in range(4):
                    p2, j = h // 2, h % 2
                    P.mm(po[:, h * C:(h + 1) * C], S[:, p2, :], qgm[:, p2, j, c * C:(c + 1) * C],
                         start=True, stop=False)
                    P.mm(po[:, h * C:(h + 1) * C], vtm[0:C, c, p2, 64 * j:64 * j + 64],
                         AT[0:C, c * 4 + h, :], start=False, stop=True)
                P.copy(ohg[:, :, c * C:(c + 1) * C], po.rearrange("p (h t) -> p h t", h=4), eng="act")
                for p2 in range(2):
                    pS = pb[2 + p2][:, 0:64]
                    P.mm(pS, kglm[0:C, c, p2, 0, :], vtm[0:C, c, p2, 0:64], start=True, stop=False)
                    P.mm(pS, kglm[0:C, c, p2, 1, :], vtm[0:C, c, p2, 64:128], start=False, stop=True)
                    P.stt(Sn[:, p2, :], S[:, p2, :], eGl[:, p2, c:c + 1], pS, ALU.mult, ALU.add)
            if samp:
                P.dma(hg_s_o[l][:, s0:s0 + nsq], Ssm)
            elif tl["last"]:
                P.dma(hg_p_o[l], (S_hg, S_hgB)[hg_cur[0]][:])
            W.reset(late_off)
            ohb = W.alloc([64, 4, TT], BF16)
            sqo = W.alloc([64, 4, TT], BF16)
            rinv = W.alloc([64, 4, TT])
            P.act(sqo, ohg, AF.Square)
            for h in range(4):
                bank = pb[4 + h // 2]
                P.mm(bank[0:64, (h % 2) * TT:(h % 2) * TT + TT], ones_b[0:64, 0:64], sqo[:, h, :])
            for hp in range(2):
                P.act(rinv[:, 2 * hp:2 * hp + 2, :],
                      pb[4 + hp][0:64, 0:2 * TT].rearrange("p (h t) -> p h t", h=2),
                      AF.Sqrt, bias=EPS, scale=1.0 / 64)
            P.recip(rinv, rinv)
            P.tt(ohg, ohg, rinv, ALU.mult)
            P.stt(ohb, ohg, hgnT[:, l:l + 1], hzs, ALU.mult, ALU.mult)
            for m in range(8):
                ps = pb[m % 4][:, 0:TT]
                for h in range(4):
                    P.mm(ps, wout[:, h, m * 128:(m + 1) * 128], ohb[:, h, :],
                         start=(h == 0), stop=(h == 3))
                x_update(l, 16, m, ps, tl)

        chk('hg%d' % l)
        for hp in range(2):
            sl = next_slot(0)
            win = sl[:, 0:8 * 1028].rearrange("p (k n) -> p k n", k=8)
            wout = sl[:, 8 * 1028:8 * 1028 + 2048].rearrange("p (h n) -> p h n", h=2)
            wsrc = w_in_d[l].rearrange("(k p) n -> p k n", p=128)
            for gi_, base in enumerate((1024, 1536, 2048, 2560)):
                P.dma(win[:, :, gi_ * 256:(gi_ + 1) * 256],
                      wsrc[:, :, base + 256 * hp:base + 256 * hp + 256], eng="pool")
            P.dma(win[:, :, 1024:1026], wsrc[:, :, 3072 + 2 * hp:3072 + 2 * hp + 2], eng="pool")
            P.dma(win[:, :, 1026:1028], wsrc[:, :, 3076 + 2 * hp:3076 + 2 * hp + 2], eng="pool")
            P.dma(wout, w_out_d[l][256 + 256 * hp:512 + 256 * hp, :].rearrange("(h e) n -> e h n", e=128),
                  eng="pool")
            chid = [2 * hp, 2 * hp + 1, 4 + 2 * hp, 5 + 2 * hp, 8 + 2 * hp, 9 + 2 * hp]
            carry = None
            W.reset()
            gst = W.alloc([128, 6, NSQ, 3])
            gso = W.alloc([128, 6, NSQ, 3])
            for i6 in range(6):
                P.dma(gst[:, i6], st_gconv_d[l][:, chid[i6]])
            base_off = W.off
            for tl in seq_tiles(256, 8):
                t0, TT, samp, nseq, T = tl["t0"], tl["TT"], tl["samp"], tl["nseq"], tl["T"]
                s0 = tl["s0"]
                C = 4 if samp else 64
                NC = TT // C
                NPR = NC * 2
                nlev = 1 if samp else 5
                W.reset(base_off)
                W2.reset()
                W_x[0] = W.alloc([128, NTS])
                carry_buf = W.alloc([128, 6, 3])
                gzs = W.alloc([128, 2, TT])
                gab = W.alloc([64, NPR, 2])
                gT = W.alloc([64, NPR])
                bT = W.alloc([64, NPR])
                nbT = W.alloc([64, NPR])
                gcs = W.alloc([64, NPR])
                egl = W.alloc([128, NPR])
                bw = W.alloc([64, NPR])
                ekl = W.alloc([64, NPR])
                Pm = [W.alloc([64, NPR, C]), None]
                PTm = [W.alloc([128, NPR, C]), None]
                dd_off = W.off
                DA = W.alloc([64, NPR, C])
                DT = W.alloc([64, NPR, C])
                W.reset(dd_off)
                Pm[1] = W.alloc([64, NPR, C])
                PTm[1] = W.alloc([64, NPR, C])
                late_off = W.off
                ext = W.alloc([128, 6, nseq, 3 + T])
                cv = W.alloc([128, 6, TT])
                sq = W.alloc([128, 4, TT])
                Sb = [W.alloc([128, 2, 128]) for _ in range(3)] if samp else None
                ru = W2.alloc([64, NPR, 128])
                rw = W2.alloc([64, NPR, 128])
                kgl = W2.alloc([64, NPR, 128])
                Wsm = W if samp else W2
                Lg = Wsm.alloc([64, NPR, C])
                attT = Wsm.alloc([64, NPR, C])
                Xm = [Wsm.alloc([64, NPR, C]) for _ in range(2)]
                nwT = Wsm.alloc([128, NPR, C])
                qg = Wsm.alloc([128, NPR, C])
                extv = ext
                bi = 0
                for i6 in range(6):
                    ps = pb[bi % 4][:, 0:TT]; bi += 1
                    proj(ps, win, i6 * 128, 128, t0, TT)
                    P.copy(ext[:, i6, :, 3:3 + T], ps.rearrange("p (s t) -> p s t", t=T), eng="act")
                for c in range(2):
                    ps = pb[bi % 4][:, 0:TT]; bi += 1
                    proj(ps, win, 768 + c * 128, 128, t0, TT)
                    P.act(gzs[:, c, :], ps, AF.Silu)
                if l == 0 and hp == 0 and tl['first']:
                    chk('g1')
                psg = pb[4]
                for c in range(NC):
                    for k in range(8):
                        P.mm(psg[0:C, c * 4:c * 4 + 4], hb[:, k, t0 + c * C:t0 + (c + 1) * C],
                             win[:, k, 1024:1028], start=(k == 0), stop=(k == 7))
                psgv = psg[0:C, 0:NC * 4].rearrange("p (c f) -> p c f", f=4)
                gabv = gab[0:C].rearrange("p (c h) f -> p c h f", h=2)
                P.copy(gabv[:, :, :, 0], psgv[:, :, 0:2])
                P.copy(gabv[:, :, :, 1], psgv[:, :, 2:4])
                if l == 0 and hp == 0 and tl['first']:
                    chk('g1b')
                if samp:
                    P.copy(ext[:, :, :, 0:3], gst[:, :, s0:s0 + nseq, :])
                elif tl["first"]:
                    P.memset(ext[:, :, :, 0:3], 0.0)
                else:
                    P.copy(ext[:, :, 0, 0:3], carry)
                cvv = cv.rearrange("p c (s t) -> p c s t", t=T)
                for i6 in range(6):
                    P.ts(cvv[:, i6], ext[:, i6, :, 0:T], gcwT[:, l, chid[i6], 0:1], ALU.mult)
                    for j in range(1, 4):
                        P.stt(cvv[:, i6], ext[:, i6, :, j:j + T], gcwT[:, l, chid[i6], j:j + 1],
                              cvv[:, i6], ALU.mult, ALU.add)
                if samp:
                    P.copy(gso[:, :, s0:s0 + nseq, :], ext[:, :, :, T:T + 3])
                else:
                    P.copy(carry_buf, ext[:, :, 0, T:T + 3])
                    if tl["last"]:
                        for i6 in range(6):
                            P.dma(gconv_p_o[l][:, chid[i6], :], carry_buf[:, i6, :])
                P.act(cv, cv, AF.Silu)
                if l == 0 and hp == 0 and tl['first']:
                    chk('g2')
                P.act(sq, cv[:, 0:4, :], AF.Square)
                for i4 in range(4):
                    bank = pb[5 + i4 // 2]
                    P.mm(bank[:, (i4 % 2) * TT:(i4 % 2) * TT + TT], ones_f[:], sq[:, i4, :])
                for i2 in range(2):
                    P.act(sq[:, 2 * i2:2 * i2 + 2, :],
                          pb[5 + i2][:, 0:2 * TT].rearrange("p (h t) -> p h t", h=2),
                          AF.Sqrt, bias=EPS)
                P.recip(sq, sq)
                P.stt(cv[:, 0:2, :], cv[:, 0:2, :], 128.0 ** -0.5, sq[:, 0:2, :], ALU.mult, ALU.mult)
                P.tt(cv[:, 2:4, :], cv[:, 2:4, :], sq[:, 2:4, :], ALU.mult)
                qT = cv[:, 0:2, :]
                kT = cv[:, 2:4, :]
                vT = cv[:, 4:6, :]
                if l == 0 and hp == 0 and tl['first']:
                    chk('g3')
                gabr = gab[0:C].rearrange("p (c h) f -> p c h f", h=2)
                gTv = gT[0:C].rearrange("p (c h) -> p c h", h=2)
                P.tt(gTv, gabr[:, :, :, 0],
                     bc(dtb[0:C, l, 2 * hp:2 * hp + 2].unsqueeze(1), [C, NC, 2]), ALU.add)
                P.act(gT[0:C], gT[0:C], AF.Exp)
                P.act(gT[0:C], gT[0:C], AF.Ln, bias=1.0)
                P.tt(gTv, gTv, bc(negA[0:C, l, 2 * hp:2 * hp + 2].unsqueeze(1), [C, NC, 2]), ALU.mult)
                P.act(bT[0:C].rearrange("p (c h) -> p c h", h=2), gabr[:, :, :, 1], AF.Sigmoid)
                P.ts(nbT[0:C], bT[0:C], -1.0, ALU.mult)
                P.tt(Lg[0:C], bc(tri[0:C, 0:C].unsqueeze(1), [C, NPR, C]),
                     bc(gT[0:C].unsqueeze(2), [C, NPR, C]), ALU.mult)
                psc = pb[4]
                P.mm(psc[0:C, 64:64 + NPR], tri[0:C, 0:C], gT[0:C, :])
                P.mm(psc[:, 128:128 + NPR], ones_f[0:C, :], gT[0:C, :])
                P.copy(gcs[0:C], psc[0:C, 64:64 + NPR], eng="act")
                P.act(egl, psc[:, 128:128 + NPR], AF.Exp)
                P.act(bw[0:C], gcs[0:C], AF.Exp)
                P.tt(bw[0:C], bw[0:C], bT[0:C], ALU.mult)
                P.tt(ekl[0:C], psc[0:C, 128:128 + NPR], gcs[0:C], ALU.subtract)
                P.act(ekl[0:C], ekl[0:C], AF.Exp)
                if l == 0 and hp == 0 and tl['first']:
                    chk('g4')
                pKK, pKQ, pDA, pDT = pb[0], pb[1], pb[2], pb[3]
                for pr in range(NPR):
                    c, hh = pr // 2, pr % 2
                    cs = slice(c * C, (c + 1) * C)
                    o = pr * C
                    P.mm(pKK[0:C, o:o + C], kT[:, hh, cs], kT[:, hh, cs])
                    P.mm(pKQ[0:C, o:o + C], kT[:, hh, cs], qT[:, hh, cs])

                def pv(bank):
                    return bank[0:C, 0:NPR * C].rearrange("p (a t) -> p a t", t=C)
                P.memset(PTm[0], 0.0)
                P.mm(pDA[0:C, 0:NPR * C], ones_f[0:C, 0:C], Lg[0:C].rearrange("p a t -> p (a t)"))
                gcb = bc(gcs[0:C].unsqueeze(2), [C, NPR, C])
                P.tt(DA[0:C], gcb, pv(pDA), ALU.subtract)
                P.tt(DA[0:C], DA[0:C], bc(nega[0:C, 0:C].unsqueeze(1), [C, NPR, C]), ALU.add)
                P.tt(DT[0:C], pv(pDA), gcb, ALU.subtract)
                P.tt(DT[0:C], DT[0:C], bc(negt[0:C, 0:C].unsqueeze(1), [C, NPR, C]), ALU.add)
                P.act(DA[0:C], DA[0:C], AF.Exp)
                P.act(DT[0:C], DT[0:C], AF.Exp)
                P.tt(PTm[0][0:C], pv(pKK), DA[0:C], ALU.mult)
                P.tt(PTm[0][0:C], PTm[0][0:C], bc(nbT[0:C].unsqueeze(2), [C, NPR, C]), ALU.mult)
                P.tt(attT[0:C], pv(pKQ), DT[0:C], ALU.mult)
                if l == 0 and hp == 0 and tl['first']:
                    chk('g5')
                for p0 in range(0, NPR, 4):
                    pP = pb[5 + (p0 // 4) % 2]
                    n4 = min(4, NPR - p0)
                    for i4 in range(n4):
                        P.tr(pP[0:C, i4 * 128:(i4 + 1) * 128], PTm[0][:, p0 + i4, :], ident[:])
                    if l == 0 and hp == 0 and tl['first']:
                        chk('g5a')
                    pPv = pP[0:C, 0:n4 * 128].rearrange("p (a e) -> p a e", e=128)[:, :, 0:C]
                    P.copy(Pm[0][0:C, p0:p0 + n4, :], pPv, eng="act")
                    if l == 0 and hp == 0 and tl['first']:
                        chk('g5c')
                    P.tt(Xm[0][0:C, p0:p0 + n4, :], Pm[0][0:C, p0:p0 + n4, :],
                         bc(ident[0:C, 0:C].unsqueeze(1), [C, n4, C]), ALU.add)
                if l == 0 and hp == 0 and tl['first']:
                    chk('g5b')
                def kv_group(p0):
                    bk, bv = pb[0], pb[1]
                    n4 = min(4, NPR - p0)
                    for i4 in range(n4):
                        pr = p0 + i4
                        c, hh = pr // 2, pr % 2
                        cs = slice(c * C, (c + 1) * C)
                        P.tr(bk[0:C, i4 * 128:(i4 + 1) * 128], kT[:, hh, cs], ident[:])
                        P.tr(bv[0:C, i4 * 128:(i4 + 1) * 128], vT[:, hh, cs], ident[:])
                    bkv = bk[0:C, 0:n4 * 128].rearrange("p (a e) -> p a e", e=128)
                    bvv = bv[0:C, 0:n4 * 128].rearrange("p (a e) -> p a e", e=128)
                    P.tt(ru[0:C, p0:p0 + n4, :], bvv, bc(bT[0:C, p0:p0 + n4].unsqueeze(2), [C, n4, 128]),
                         ALU.mult)
                    P.tt(rw[0:C, p0:p0 + n4, :], bkv, bc(bw[0:C, p0:p0 + n4].unsqueeze(2), [C, n4, 128]),
                         ALU.mult)
                    P.tt(kgl[0:C, p0:p0 + n4, :], bkv, bc(ekl[0:C, p0:p0 + n4].unsqueeze(2), [C, n4, 128]),
                         ALU.mult)

                def e_block():
                    pE = pb[3]
                    P.mm(pE[:, 0:NPR * C], ones_f[0:C, :], Lg[0:C].rearrange("p a t -> p (a t)"))
                    P.act(qg, pE[:, 0:NPR * C].rearrange("p (a t) -> p a t", t=C), AF.Exp)
                    qgv = qg.rearrange("p (c h) t -> p h c t", h=2)
                    P.tt(qgv, qgv, qT.rearrange("p h (c t) -> p h c t", t=C), ALU.mult)

                pending = [(lambda p0=p0: kv_group(p0)) for p0 in range(0, NPR, 4)] + [e_block]
                cur = 0
                for lev in range(1, nlev + 1):
                    nxt = 1 - cur
                    last = (lev == nlev)
                    pA_, pB_, pC_ = pb[5], pb[6], pb[7]
                    if not last:
                        for pr in range(NPR):
                            P.mm(pA_[0:C, pr * C:(pr + 1) * C], PTm[cur][0:C, pr, :], Pm[cur][0:C, pr, :])
                    for pr in range(NPR):
                        P.mm(pB_[0:C, pr * C:(pr + 1) * C], Pm[cur][0:C, pr, :], PTm[cur][0:C, pr, :])
                    if not last:
                        P.copy(Pm[nxt][0:C], pv(pA_), eng="act")
                    P.copy(PTm[nxt][0:C], pv(pB_))
                    if pending:
                        pending.pop(0)()
                    for pr in range(NPR):
                        P.mm(pC_[0:C, pr * C:(pr + 1) * C], PTm[nxt][0:C, pr, :], Xm[cur][0:C, pr, :],
                             start=True, stop=False)
                        P.mm(pC_[0:C, pr * C:(pr + 1) * C], ident[0:C, 0:C], Xm[cur][0:C, pr, :],
                             start=False, stop=True)
                    P.copy(Xm[nxt][0:C], pv(pC_), eng="act")
                    cur = nxt
                X = Xm[cur]
                if l == 0 and hp == 0 and tl['first']:
                    chk('g6')
                for th in pending:
                    th()
                pW = pb[2]
                for pr in range(NPR):
                    P.mm(pW[:, pr * C:(pr + 1) * C], rw[0:C, pr, :], X[0:C, pr, :])
                P.act(nwT, pW[:, 0:NPR * C].rearrange("p (a t) -> p a t", t=C), AF.Copy, scale=-1.0)
                if not samp:
                    W.reset(late_off)
                vnew = [W.alloc([64, 128]) for _ in range(4)]
                ogd = W.alloc([128, 2, TT])
                ogb = W.alloc([128, 2, TT], BF16)
                sqo = W.alloc([128, 2, TT], BF16)
                rinv = W.alloc([128, 2, TT])
                if (not samp) and tl["first"]:
                    P.memset(S_gd[:], 0.0)
                vi = 0
                for c in range(NC):
                    if samp:
                        S = Sb[c % 3]
                        P.dma(S, st_gdn_d[l, s0 + c, 2 * hp:2 * hp + 2].rearrange("h d e -> d h e"))
                    else:
                        S = S_gd
                    pO = pb[4 + (c % 2)]
                    vns = []
                    for hh in range(2):
                        pr = c * 2 + hh
                        pV = pb[6 + hh][0:C, 0:128]
                        P.mm(pV, X[0:C, pr, :], ru[0:C, pr, :], start=True, stop=False)
                        P.mm(pV, nwT[:, pr, :], S[:, hh, :], start=False, stop=True)
                    for hh in range(2):
                        pV = pb[6 + hh][0:C, 0:128]
                        vn = vnew[vi % 4]; vi += 1
                        vns.append(vn)
                        P.copy(vn[0:C, :], pV, eng="act")
                    for hh in range(2):
                        pr = c * 2 + hh
                        vn = vns[hh]
                        P.mm(pO[:, hh * C:(hh + 1) * C], S[:, hh, :], qg[:, pr, :], start=True, stop=False)
                        P.mm(pO[:, hh * C:(hh + 1) * C], vn[0:C, :], attT[0:C, pr, :], start=False, stop=True)
                    for hh in range(2):
                        pr = c * 2 + hh
                        pS = pb[6 + hh][:, 256:384]
                        P.mm(pS, kgl[0:C, pr, :], vns[hh][0:C, :])
                    for hh in range(2):
                        pr = c * 2 + hh
                        pS = pb[6 + hh][:, 256:384]
                        P.stt(S[:, hh, :], S[:, hh, :], egl[:, pr:pr + 1], pS, ALU.mult, ALU.add)
                    P.copy(ogd[:, :, c * C:(c + 1) * C],
                           pO[:, 0:2 * C].rearrange("p (h t) -> p h t", h=2), eng="act")
                    if samp:
                        P.dma(gdn_s_o[l, s0 + c, 2 * hp:2 * hp + 2].rearrange("h d e -> d h e"), S)
                if (not samp) and tl["last"]:
                    P.dma(gdn_p_o[l, 2 * hp:2 * hp + 2].rearrange("h d e -> d h e"), S_gd[:])
                carry = carry_buf
                if l == 0 and hp == 0 and tl['first']:
                    chk('g9')
                P.act(sqo, ogd, AF.Square)
                pm = pb[0]
                for hh in range(2):
                    P.mm(pm[:, hh * TT:(hh + 1) * TT], ones_b[:], sqo[:, hh, :])
                P.act(rinv, pm[:, 0:2 * TT].rearrange("p (h t) -> p h t", h=2), AF.Sqrt, bias=EPS,
                      scale=1.0 / 128)
                P.recip(rinv, rinv)
                P.tt(ogd, ogd, rinv, ALU.mult)
                P.stt(ogb, ogd, gnT[:, l:l + 1], gzs, ALU.mult, ALU.mult)
                for m in range(8):
                    ps = pb[1 + m % 3][:, 0:TT]
                    for hh in range(2):
                        P.mm(ps, wout[:, hh, m * 128:(m + 1) * 128], ogb[:, hh, :],
                             start=(hh == 0), stop=(hh == 1))
                    x_update(l, 16, m, ps, tl)
            for i6 in range(6):
                P.dma(gconv_s_o[l][:, chid[i6]], gso[:, i6])

        chk('gd%d' % l)
        sl = next_slot(0)
        win = sl[:, 0:4096].rearrange("p (k n) -> p k n", k=8)
        wout = sl[:, 4096:6144].rearrange("p (h n) -> p h n", h=2)
        P.dma(win, w_in_d[l].rearrange("(k p) n -> p k n", p=128)[:, :, 3080:3592], eng="pool")
        P.dma(wout, w_out_d[l][768:1024, :].rearrange("(h e) n -> e h n", e=128), eng="pool")
        carry = None
        W.reset()
        lst = W.alloc([128, 2, NSQ, 3])
        lso = W.alloc([128, 2, NSQ, 3])
        P.dma(lst, st_lconv_d[l])
        base_off = W.off
        for tl in seq_tiles(512):
            t0, TT, samp, nseq, T = tl["t0"], tl["TT"], tl["samp"], tl["nseq"], tl["T"]
            W.reset(base_off)
            W2.reset()
            W_x[0] = W.alloc([128, NTS])
            carry_buf = W.alloc([128, 2, 3])
            ext = W.alloc([128, 2, nseq, 3 + T])
            xc = W.alloc([128, 2, TT])
            glz = W.alloc([128, 2, TT])
            rr = W2.alloc([128, 2, TT])
            ii = W2.alloc([128, 2, TT])
            aa = W2.alloc([128, 2, TT])
            bb = W.alloc([128, 2, TT])
            hh_ = W.alloc([128, 2, TT])
            olb = W.alloc([128, 2, TT], BF16)
            h0s = W.alloc([128, 2, NSQ]) if samp else None
            h1s = W.alloc([128, 2, NSQ]) if samp else None
            for c in range(2):
                ps = pb[c][:, 0:TT]
                proj(ps, win, c * 128, 128, t0, TT)
                P.copy(ext[:, c, :, 3:3 + T], ps.rearrange("p (s t) -> p s t", t=T), eng="act")
            for c in range(2):
                ps = pb[2 + c][:, 0:TT]
                proj(ps, win, 256 + c * 128, 128, t0, TT)
                P.act(glz[:, c, :], ps, AF.Gelu_apprx_tanh)
            if samp:
                P.copy(ext[:, :, :, 0:3], lst)
                P.dma(h0s, st_lru_d[l])
            elif tl["first"]:
                P.memset(ext[:, :, :, 0:3], 0.0)
                P.memset(h_lru[:], 0.0)
            else:
                P.copy(ext[:, :, 0, 0:3], carry)
            xcv = xc.rearrange("p c (s t) -> p c s t", t=T)
            for c in range(2):
                P.ts(xcv[:, c], ext[:, c, :, 0:T], lcwT[:, l, c, 0:1], ALU.mult,
                     lcbT[:, l, c:c + 1], ALU.add)
                for j in range(1, 4):
                    P.stt(xcv[:, c], ext[:, c, :, j:j + T], lcwT[:, l, c, j:j + 1], xcv[:, c],
                          ALU.mult, ALU.add)
            if samp:
                P.copy(lso, ext[:, :, :, T:T + 3])
                P.dma(lconv_s_o[l], lso)
            else:
                P.copy(carry_buf, ext[:, :, 0, T:T + 3])
                if tl["last"]:
                    P.dma(lconv_p_o[l], carry_buf)
            carry = carry_buf
            for c in range(2):
                pr_ = pb[4 + c][:, 0:TT]
                pi_ = pb[6 + c][:, 0:TT]
                P.mm(pr_, wabd[:, l, c, :], xc[:, c, :])
                P.mm(pi_, wxbd[:, l, c, :], xc[:, c, :])
                P.act(rr[:, c, :], pr_, AF.Sigmoid, bias=lbaT[:, l, c:c + 1])
                P.act(ii[:, c, :], pi_, AF.Sigmoid, bias=lbxT[:, l, c:c + 1])
                P.act(aa[:, c, :], rr[:, c, :], AF.Exp, scale=nl8[:, l, c:c + 1])
            P.tt(bb, aa, aa, ALU.mult)
            P.ts(bb, bb, -1.0, ALU.mult, 1.0, ALU.add)
            P.act(bb, bb, AF.Sqrt)
            P.tt(ii, ii, xc, ALU.mult)
            P.tt(bb, bb, ii, ALU.mult)
            for c in range(2):
                if not samp:
                    P.scan(hh_[:, c, :], aa[:, c, :], bb[:, c, :], h_lru[:, c:c + 1])
                    P.copy(h_lru[:, c:c + 1], hh_[:, c, TT - 1:TT])
                else:
                    for s in range(NSQ):
                        P.scan(hh_[:, c, s * T:(s + 1) * T], aa[:, c, s * T:(s + 1) * T],
                               bb[:, c, s * T:(s + 1) * T], h0s[:, c, s:s + 1])
            if samp:
                P.copy(h1s, hh_.rearrange("p c (s t) -> p c s t", t=T)[:, :, :, T - 1])
                P.dma(lru_s_o[l], h1s)
            elif tl["last"]:
                P.dma(lru_p_o[l], h_lru[:])
            P.tt(olb, hh_, glz, ALU.mult)
            for m in range(8):
                ps = pb[m % 4][:, 0:TT]
                for c in range(2):
                    P.mm(ps, wout[:, c, m * 128:(m + 1) * 128], olb[:, c, :],
                         start=(c == 0), stop=(c == 1))
                x_update(l, 16, m, ps, tl)

        chk('lru%d' % l)
        modnorm(l, nffnT, 32, 24)
        parts = [(0, 4), (4, 8), (8, 12), (12, 16), (16, 19), (19, 22)]
        W.reset()
        W_x[0] = W.alloc([128, NTS])
        X_tmp[0] = [W.alloc([128, 512]) for _ in range(2)]
        fst = W.alloc([128, NFC, NSQ, 2])
        fout = W.alloc([128, NFC, NSQ, 2])
        P.dma(fst, st_fconv_d[l])
        base_off = W.off

        def load_part(f0, f1):
            nf = f1 - f0
            sl = next_slot()
            wg = sl[:, 0:8 * nf * 128].rearrange("p (k n) -> p k n", k=8)
            wu = sl[:, 4096:4096 + 8 * nf * 128].rearrange("p (k n) -> p k n", k=8)
            wd = sl[:, 8192:8192 + nf * 1024].rearrange("p (f n) -> p f n", f=nf)
            P.dma(wg, w_gate_d[l].rearrange("(k p) n -> p k n", p=128)[:, :, f0 * 128:f1 * 128], eng="pool")
            P.dma(wu, w_up_d[l].rearrange("(k p) n -> p k n", p=128)[:, :, f0 * 128:f1 * 128], eng="pool")
            P.dma(wd, w_down_d[l][f0 * 128:f1 * 128, :].rearrange("(f p) n -> p f n", p=128), eng="pool")
            return wg, wu, wd

        nxt_w = load_part(*parts[0])
        for pi, (f0, f1) in enumerate(parts):
            nf = f1 - f0
            wg, wu, wd = nxt_w
            if pi + 1 < len(parts):
                nxt_w = load_part(*parts[pi + 1])
            for tl in seq_tiles(512):
                t0, TT, samp, nseq, T = tl["t0"], tl["TT"], tl["samp"], tl["nseq"], tl["T"]
                W.reset(base_off)
                gext = [W.alloc([128, nseq, 2 + T]) for _ in range(2)]
                acc = [W.alloc([128, nseq, T]) for _ in range(2)]
                mt = W.alloc([128, nf, TT], BF16)
                for fi in range(nf):
                    fg = f0 + fi
                    psG = pb[(fi % 2)][:, 0:TT]
                    psU = pb[2 + (fi % 2)][:, 0:TT]
                    for k in range(8):
                        P.mm(psG, wg[:, k, fi * 128:(fi + 1) * 128], hb[:, k, t0:t0 + TT],
                             start=(k == 0), stop=(k == 7))
                    for k in range(8):
                        P.mm(psU, wu[:, k, fi * 128:(fi + 1) * 128], hb[:, k, t0:t0 + TT],
                             start=(k == 0), stop=(k == 7))
                    ge = gext[fi % 2]
                    ac = acc[fi % 2]
                    P.copy(ge[:, :, 2:2 + T], psG.rearrange("p (s t) -> p s t", t=T), eng="act")
                    if samp:
                        P.copy(ge[:, :, 0:2], fst[:, fg, :, :], eng="pool")
                    elif tl["first"]:
                        P.memset(ge[:, :, 0:2], 0.0, eng="pool")
                    else:
                        P.copy(ge[:, 0, 0:2], fcarry[:, fg, :], eng="pool")
                    P.act(ac, ge[:, :, 0:T], AF.Identity, bias=fcbT[:, l, fg:fg + 1],
                          scale=fcwT[:, l, fg, 0:1])
                    P.stt(ac, ge[:, :, 1:1 + T], fcwT[:, l, fg, 1:2], ac, ALU.mult, ALU.add)
                    P.stt(ac, ge[:, :, 2:2 + T], fcwT[:, l, fg, 2:3], ac, ALU.mult, ALU.add)
                    if samp:
                        P.copy(fout[:, fg, :, :], ge[:, :, T:T + 2], eng="pool")
                    else:
                        P.copy(fcarry[:, fg, :], ge[:, 0, T:T + 2], eng="pool")
                    P.act(ac, ac, AF.Silu)
                    P.tt(mt[:, fi, :], ac.rearrange("p s t -> p (s t)"), psU, ALU.mult)
                for m in range(8):
                    ps = pb[4 + m % 4][:, 0:TT]
                    for fi in range(nf):
                        P.mm(ps, wd[:, fi, m * 128:(m + 1) * 128], mt[:, fi, :],
                             start=(fi == 0), stop=(fi == nf - 1))
                    x_update(l, 40, m, ps, tl)
        X_tmp[0] = None
        P.dma(fconv_s_o[l], fout)
        P.dma(fconv_p_o[l], fcarry[:])

    chk('ffn')
    for tl in seq_tiles(256):
        t0, TT = tl["t0"], tl["TT"]
        W.reset()
        sq = W.alloc([128, 8, TT], BF16)
        tmp = W.alloc([128, 8, TT])
        rs = W.alloc([128, TT])
        P.act(sq, x[:, :, t0:t0 + TT], AF.Square)
        ps = pb[0][:, 0:TT]
        for k in range(8):
            P.mm(ps, ones_b[:], sq[:, k, :], start=(k == 0), stop=(k == 7))
        P.act(rs, ps, AF.Sqrt, bias=EPS, scale=1.0 / D)
        P.recip(rs, rs)
        P.tt(tmp, x[:, :, t0:t0 + TT], bc(rs.unsqueeze(1), [128, 8, TT]), ALU.mult)
        P.tt(tmp, tmp, bc(fnormT[:].unsqueeze(2), [128, 8, TT]), ALU.mult)
        P.dma(yT_o.rearrange("(k p) t -> p k t", p=128)[:, :, t0:t0 + TT], tmp)


_CACHE = {}


def _prep_inputs(inp):
    f = lambda a: np.ascontiguousarray(np.asarray(a, dtype=np.float32))
    xp = f(inp["x_prompt"]); xs = f(inp["x_sample"])
    cp = f(inp["c_prompt"]); cs = f(inp["c_sample"])

    def fm(v, nch):
        v = f(v)
        return np.ascontiguousarray(np.swapaxes(v.reshape(v.shape[:-1] + (nch, 128)), -1, -2))

    shared = {
        "w_ada": f(inp["w_ada"]), "b_adaT": fm(inp["b_ada"], 48),
        "w_in": f(inp["w_in"]), "w_out": f(inp["w_out"]),
        "w_gate": f(inp["ffn_w_gate"]), "w_up": f(inp["ffn_w_up"]), "w_down": f(inp["ffn_w_down"]),
        "nmixT": fm(inp["norm_mix"], 8), "nffnT": fm(inp["norm_ffn"], 8),
        "fnormT": fm(inp["final_norm"], 8),
        "hlbT": np.ascontiguousarray(f(inp["hg_lower_bound"]).reshape(2, 2, 128).transpose(0, 2, 1)),
        "hgnT": np.ascontiguousarray(f(inp["hg_norm"]).T),
        "gcwT": np.ascontiguousarray(f(inp["gdn_conv_w"]).reshape(2, 4, 12, 128).transpose(0, 3, 2, 1)),
        "galog": f(inp["gdn_a_log"]), "gdtb": f(inp["gdn_dt_bias"]),
        "gnT": np.ascontiguousarray(f(inp["gdn_norm"]).T),
        "lcwT": np.ascontiguousarray(f(inp["lru_conv_w"]).reshape(2, 4, 2, 128).transpose(0, 3, 2, 1)),
        "lcbT": fm(inp["lru_conv_b"], 2),
        "lwa": f(inp["lru_w_a"]), "lwx": f(inp["lru_w_x"]),
        "lbaT": fm(inp["lru_b_a"], 2), "lbxT": fm(inp["lru_b_x"], 2), "llamT": fm(inp["lru_lambda"], 2),
        "fcwT": np.ascontiguousarray(f(inp["ffn_conv_w"]).reshape(2, 3, NFC, 128).transpose(0, 3, 2, 1)),
        "fcbT": fm(inp["ffn_conv_b"], NFC),
    }
    s_hg = f(inp["state_hgrn"]); s_gd = f(inp["state_gdn"]); s_gc = f(inp["state_gdn_conv"])
    s_lr = f(inp["state_lru"]); s_lc = f(inp["state_lru_conv"]); s_fc = f(inp["state_ffn_conv"])
    maps = []
    for i in range(NCORES):
        sl = slice(NSQ * i, NSQ * (i + 1))
        xT = np.concatenate([xp[i].T, xs[sl].reshape(NTS, D).T], axis=1)
        cT = np.concatenate([cp[i][:, None], cs[sl].T], axis=1)
        m = dict(shared)
        m["xT"] = np.ascontiguousarray(xT)
        m["cT"] = np.ascontiguousarray(cT)
        m["st_hg"] = np.ascontiguousarray(
            s_hg[:, sl].reshape(2, NSQ, 2, 2, 64, 64).transpose(0, 3, 4, 1, 2, 5).reshape(2, 128, NSQ, 2, 64))
        m["st_gdn"] = np.ascontiguousarray(s_gd[:, sl])
        m["st_gconv"] = np.ascontiguousarray(s_gc[:, sl].reshape(2, NSQ, 3, 12, 128).transpose(0, 4, 3, 1, 2))
        m["st_lru"] = np.ascontiguousarray(s_lr[:, sl].reshape(2, NSQ, 2, 128).transpose(0, 3, 2, 1))
        m["st_lconv"] = np.ascontiguousarray(s_lc[:, sl].reshape(2, NSQ, 3, 2, 128).transpose(0, 4, 3, 1, 2))
        m["st_fconv"] = np.ascontiguousarray(s_fc[:, sl].reshape(2, NSQ, 2, NFC, 128).transpose(0, 4, 3, 1, 2))
        maps.append(m)
    return maps


def kernel(**inputs):
    if "nc" not in _CACHE:
        _CACHE["nc"] = build_program()
    nc = _CACHE["nc"]
    maps = _prep_inputs(inputs)
    res = run_bass_kernel_spmd(nc, maps, core_ids=list(range(NCORES)))
    R = res.results
    B = NCORES
    y_p = np.empty((B, TP, D), np.float32)
    y_s = np.empty((B * NSQ, TSQ, D), np.float32)
    hg_p = np.empty((2, B, 4, 64, 64), np.float32)
    gdn_p = np.empty((2, B, 4, 128, 128), np.float32)
    gconv_p = np.empty((2, B, 3, 1536), np.float32)
    lru_p = np.empty((2, B, 256), np.float32)
    lconv_p = np.empty((2, B, 3, 256), np.float32)
    fconv_p = np.empty((2, B, 2, DFF), np.float32)
    hg_s = np.empty((2, B * NSQ, 4, 64, 64), np.float32)
    gdn_s = np.empty((2, B * NSQ, 4, 128, 128), np.float32)
    gconv_s = np.empty((2, B * NSQ, 3, 1536), np.float32)
    lru_s = np.empty((2, B * NSQ, 256), np.float32)
    lconv_s = np.empty((2, B * NSQ, 3, 256), np.float32)
    fconv_s = np.empty((2, B * NSQ, 2, DFF), np.float32)
    for i in range(B):
        r = R[i]
        sl = slice(NSQ * i, NSQ * (i + 1))
        yT = np.asarray(r["yT"])
        y_p[i] = yT[:, :TP].T
        y_s[sl] = yT[:, TP:].T.reshape(NSQ, TSQ, D)
        hg_p[:, i] = np.asarray(r["o_hg_p"]).reshape(2, 2, 64, 2, 64).transpose(0, 3, 1, 2, 4).reshape(2, 4, 64, 64)
        gdn_p[:, i] = np.asarray(r["o_gdn_p"])
        gconv_p[:, i] = np.asarray(r["o_gconv_p"]).transpose(0, 3, 2, 1).reshape(2, 3, 1536)
        lru_p[:, i] = np.asarray(r["o_lru_p"]).transpose(0, 2, 1).reshape(2, 256)
        lconv_p[:, i] = np.asarray(r["o_lconv_p"]).transpose(0, 3, 2, 1).reshape(2, 3, 256)
        fconv_p[:, i] = np.asarray(r["o_fconv_p"]).transpose(0, 3, 2, 1).reshape(2, 2, DFF)
        hg_s[:, sl] = np.asarray(r["o_hg_s"]).reshape(2, 2, 64, NSQ, 2, 64).transpose(0, 3, 4, 1, 2, 5).reshape(2, NSQ, 4, 64, 64)
        gdn_s[:, sl] = np.asarray(r["o_gdn_s"])
        gconv_s[:, sl] = np.asarray(r["o_gconv_s"]).transpose(0, 3, 4, 2, 1).reshape(2, NSQ, 3, 1536)
        lru_s[:, sl] = np.asarray(r["o_lru_s"]).transpose(0, 3, 2, 1).reshape(2, NSQ, 256)
        lconv_s[:, sl] = np.asarray(r["o_lconv_s"]).transpose(0, 3, 4, 2, 1).reshape(2, NSQ, 3, 256)
        fconv_s[:, sl] = np.asarray(r["o_fconv_s"]).transpose(0, 3, 4, 2, 1).reshape(2, NSQ, 2, DFF)
    return (y_p, y_s, hg_p, gdn_p, gconv_p, lru_p, lconv_p, fconv_p,
            hg_s, gdn_s, gconv_s, lru_s, lconv_s, fconv_s)
```

```python
import numpy as np
import concourse.bass as bass
import concourse.mybir as mybir
from concourse.bass_utils import run_bass_kernel_spmd

F32 = mybir.dt.float32
BF16 = mybir.dt.bfloat16
AF = mybir.ActivationFunctionType
ALU = mybir.AluOpType

NCORES = 8
D = 1024
TP = 2048
NSQ = 16
TSQ = 4
NTS = NSQ * TSQ
NT = TP + NTS
DFF = 2816
NFC = 22
DIN = 3592
EPS = 1e-6
SLOT = 12288


class Op:
    __slots__ = ("eng", "fn", "deps", "dma", "signal", "sem", "val", "guard")

    def __init__(self, eng, fn, dma):
        self.eng = eng
        self.fn = fn
        self.deps = []
        self.dma = dma
        self.signal = dma
        self.sem = None
        self.val = 0
        self.guard = None


def _box(ap):
    name = ap.tensor.name
    pat = ap.ap
    off = ap.offset
    sz = mybir.dt.size(ap.dtype)
    space = str(ap.space)
    if space == "PSUM":
        return (name, 0, 128, 0, 2048, True)
    if space in ("SB", "PSUM"):
        pstep, pcnt = pat[0]
        if pstep == 0:
            p0 = 0
            f0 = off
        else:
            p0 = off // pstep
            f0 = off - p0 * pstep
        ext = 0
        foot = 1
        for st, cn in pat[1:]:
            ext += abs(st) * (cn - 1)
            if st != 0:
                foot *= cn
        return (name, p0, p0 + pcnt, f0 * sz, (f0 + ext + 1) * sz, foot == ext + 1)
    ext = 0
    foot = 1
    for st, cn in pat:
        ext += abs(st) * (cn - 1)
        if st != 0:
            foot *= cn
    return (name, 0, 1, off * sz, (off + ext + 1) * sz, foot == ext + 1)


def _ov(a, b):
    return a[1] < b[2] and b[1] < a[2] and a[3] < b[4] and b[3] < a[4]


def _cov(b, r):
    return b[1] <= r[1] and r[2] <= b[2] and b[3] <= r[3] and r[4] <= b[4]


class Prog:
    ENGS = ("pe", "act", "dve", "pool", "sp")

    def __init__(self, nc):
        self.nc = nc
        self.ops = {e: [] for e in self.ENGS}
        self.recs = {}
        self.all_dma = []

    def add(self, eng, fn, reads, writes, dma=False):
        op = Op(eng, fn, dma)
        rboxes = [_box(a) for a in reads]
        wboxes = [_box(a) for a in writes]
        raw = set()
        other = set()
        for b in rboxes:
            ws, rs = self.recs.setdefault(b[0], ([], []))
            for (wb, wop) in ws:
                if _ov(wb, b):
                    raw.add(wop)
        for b in wboxes:
            ws, rs = self.recs.setdefault(b[0], ([], []))
            for (wb, wop) in ws:
                if _ov(wb, b):
                    other.add(wop)
            for (rb, rop) in rs:
                if _ov(rb, b):
                    other.add(rop)
        for d in raw | other:
            if d is op:
                continue
            if (not d.dma) and (not dma) and d.eng == eng and eng == "pe":
                continue
            op.deps.append(d)
        for b in rboxes:
            ws, rs = self.recs[b[0]]
            if not dma:
                rs[:] = [(rb, rop) for (rb, rop) in rs
                         if not (rop.eng == eng and not rop.dma and _cov(b, rb))]
            rs.append((b, op))
        for b in wboxes:
            ws, rs = self.recs[b[0]]
            if b[5]:
                ws[:] = [(wb, wop) for (wb, wop) in ws if not _cov(b, wb)]
                rs[:] = [(rb, rop) for (rb, rop) in rs if not _cov(b, rb)]
            ws.append((b, op))
        self.ops[eng].append(op)
        if dma:
            self.all_dma.append(op)
        return op

    def mm(self, out, lhsT, rhs, start=True, stop=True):
        return self.add("pe", lambda e: e.matmul(out, lhsT, rhs, start=start, stop=stop),
                        [lhsT, rhs], [out])

    def tr(self, out, in_, ident):
        return self.add("pe", lambda e: e.transpose(out, in_, ident), [in_, ident], [out])

    def act(self, out, in_, func, bias=None, scale=None):
        kw = {}
        rd = [in_]
        if bias is not None:
            kw["bias"] = bias
            if not isinstance(bias, (int, float)):
                rd.append(bias)
        if scale is not None:
            kw["scale"] = scale
            if not isinstance(scale, (int, float)):
                rd.append(scale)
        return self.add("act", lambda e: e.activation(out, in_, func, **kw), rd, [out])

    def tt(self, out, in0, in1, op, eng="dve"):
        return self.add(eng, lambda e: e.tensor_tensor(out, in0, in1, op), [in0, in1], [out])

    def ts(self, out, in0, s1, op0, s2=None, op1=None, eng="dve"):
        rd = [in0]
        if not isinstance(s1, (int, float)):
            rd.append(s1)
        if s2 is not None and not isinstance(s2, (int, float)):
            rd.append(s2)
        if op1 is None:
            return self.add(eng, lambda e: e.tensor_scalar(out, in0, s1, None, op0), rd, [out])
        return self.add(eng, lambda e: e.tensor_scalar(out, in0, s1, s2, op0, op1), rd, [out])

    def stt(self, out, in0, scalar, in1, op0, op1):
        rd = [in0, in1]
        if not isinstance(scalar, (int, float)):
            rd.append(scalar)
        return self.add("dve", lambda e: e.scalar_tensor_tensor(out, in0, scalar, in1, op0, op1),
                        rd, [out])

    def scan(self, out, d0, d1, init):
        rd = [d0, d1]
        if not isinstance(init, (int, float)):
            rd.append(init)
        return self.add("dve", lambda e: e.tensor_tensor_scan(out, d0, d1, init, ALU.mult, ALU.add),
                        rd, [out])

    def copy(self, out, in_, eng="dve"):
        if eng == "act":
            return self.add("act", lambda e: e.copy(out, in_), [in_], [out])
        return self.add(eng, lambda e: e.tensor_copy(out, in_), [in_], [out])

    def memset(self, out, val, eng="dve"):
        return self.add(eng, lambda e: e.memset(out, val), [], [out])

    def recip(self, out, in_):
        return self.add("dve", lambda e: e.reciprocal(out, in_), [in_], [out])

    def dma(self, out, in_, eng="sp"):
        return self.add(eng, lambda e: e.dma_start(out, in_), [in_], [out], dma=True)

    def emit(self, n_dma_sems=(("sp", 56), ("pool", 32))):
        nc = self.nc
        for e in self.ENGS:
            for op in self.ops[e]:
                for d in op.deps:
                    d.signal = True
        esem = {e: nc.alloc_semaphore(name="s_" + e) for e in ("pe", "act", "dve", "pool")}
        pools = {e: [nc.alloc_semaphore(name="d_%s_%d" % (e, i)) for i in range(n)]
                 for e, n in n_dma_sems}
        for e in self.ENGS:
            cnt = 0
            k = 0
            for op in self.ops[e]:
                if op.dma:
                    pool = pools[e]
                    i = k % len(pool)
                    op.sem = pool[i]
                    op.val = 16 * (k // len(pool) + 1)
                    if k >= len(pool):
                        op.guard = (pool[i], op.val - 16)
                    k += 1
                elif op.signal:
                    cnt += 1
                    op.sem = esem[e]
                    op.val = cnt
        final = {}
        for op in self.all_dma:
            key = id(op.sem)
            if final.get(key, (None, 0))[1] < op.val:
                final[key] = (op.sem, op.val)
        prog = self

        def emit_engine(eng_obj, e):
            waited = {}
            for op in prog.ops[e]:
                need = {}
                for d in op.deps:
                    key = id(d.sem)
                    if need.get(key, (None, 0))[1] < d.val:
                        need[key] = (d.sem, d.val)
                if op.guard is not None:
                    key = id(op.guard[0])
                    if need.get(key, (None, 0))[1] < op.guard[1]:
                        need[key] = op.guard
                for key, (sem, val) in need.items():
                    if waited.get(key, 0) < val:
                        eng_obj.wait_ge(sem, val)
                        waited[key] = val
                ins = op.fn(eng_obj)
                if op.dma:
                    ins.then_inc(op.sem, 16)
                elif op.signal:
                    ins.then_inc(op.sem, 1)
            if e == "sp":
                for key, (sem, val) in final.items():
                    if waited.get(key, 0) < val:
                        eng_obj.wait_ge(sem, val)

        with nc.Block() as block:
            @block.tensor
            def _(eng):
                emit_engine(eng, "pe")

            @block.scalar
            def _(eng):
                emit_engine(eng, "act")

            @block.vector
            def _(eng):
                emit_engine(eng, "dve")

            @block.gpsimd
            def _(eng):
                emit_engine(eng, "pool")

            @block.sync
            def _(eng):
                emit_engine(eng, "sp")


class Arena:
    def __init__(self, nc, name, nbytes, tensor=None, base=0):
        self.t = tensor if tensor is not None else nc.alloc_sbuf_tensor(name, [128, nbytes // 4], F32)
        self.base = base
        self.nbytes = nbytes
        self.off = 0
        self.peak = 0

    def reset(self, off=0):
        self.off = off

    def alloc(self, shape, dtype=F32):
        sz = mybir.dt.size(dtype)
        n = 1
        for s in shape[1:]:
            n *= s
        nb = (n * sz + 31) // 32 * 32
        assert self.off + nb <= self.nbytes, ("arena overflow", self.off, nb, self.nbytes)
        base = self.t if dtype == self.t.dtype else self.t.bitcast(dtype)
        e0 = (self.base + self.off) // sz
        v = base[0:shape[0], e0:e0 + n]
        self.off += nb
        self.peak = max(self.peak, self.off)
        if len(shape) > 2:
            names = "abcdefg"[: len(shape) - 1]
            pat = "p (%s) -> p %s" % (" ".join(names), " ".join(names))
            v = v.rearrange(pat, **{names[i]: shape[i + 1] for i in range(len(shape) - 1)})
        return v


def bc(ap, shape):
    return ap.to_broadcast(list(shape))


class _Stop(Exception):
    pass


def build_program(stop=None):
    nc = bass.Bass("TRN2", target_bir_lowering=False)
    P = Prog(nc)
    try:
        _body(nc, P, stop)
    except _Stop:
        pass
    P.emit()
    return nc


def _body(nc, P, stop):
    def chk(tag):
        if stop == tag:
            raise _Stop()

    def din(name, shape):
        return nc.dram_tensor(name, list(shape), F32, kind="ExternalInput").ap()

    def dout(name, shape):
        return nc.dram_tensor(name, list(shape), F32, kind="ExternalOutput").ap()

    xT_d = din("xT", [D, NT])
    cT_d = din("cT", [D, 17])
    st_hg_d = din("st_hg", [2, 128, NSQ, 2, 64])
    st_gdn_d = din("st_gdn", [2, NSQ, 4, 128, 128])
    st_gconv_d = din("st_gconv", [2, 128, 12, NSQ, 3])
    st_lru_d = din("st_lru", [2, 128, 2, NSQ])
    st_lconv_d = din("st_lconv", [2, 128, 2, NSQ, 3])
    st_fconv_d = din("st_fconv", [2, 128, NFC, NSQ, 2])
    w_ada_d = din("w_ada", [2, D, 6 * D])
    b_adaT_d = din("b_adaT", [2, 128, 48])
    w_in_d = din("w_in", [2, D, DIN])
    w_out_d = din("w_out", [2, D, D])
    w_gate_d = din("w_gate", [2, D, DFF])
    w_up_d = din("w_up", [2, D, DFF])
    w_down_d = din("w_down", [2, DFF, D])
    nmixT_d = din("nmixT", [2, 128, 8])
    nffnT_d = din("nffnT", [2, 128, 8])
    fnormT_d = din("fnormT", [128, 8])
    hlbT_d = din("hlbT", [2, 128, 2])
    hgnT_d = din("hgnT", [64, 2])
    gcwT_d = din("gcwT", [2, 128, 12, 4])
    galog_d = din("galog", [2, 4])
    gdtb_d = din("gdtb", [2, 4])
    gnT_d = din("gnT", [128, 2])
    lcwT_d = din("lcwT", [2, 128, 2, 4])
    lcbT_d = din("lcbT", [2, 128, 2])
    lwa_d = din("lwa", [2, 4, 64, 64])
    lwx_d = din("lwx", [2, 4, 64, 64])
    lbaT_d = din("lbaT", [2, 128, 2])
    lbxT_d = din("lbxT", [2, 128, 2])
    llamT_d = din("llamT", [2, 128, 2])
    fcwT_d = din("fcwT", [2, 128, NFC, 3])
    fcbT_d = din("fcbT", [2, 128, NFC])

    yT_o = dout("yT", [D, NT])
    hg_p_o = dout("o_hg_p", [2, 128, 2, 64])
    gdn_p_o = dout("o_gdn_p", [2, 4, 128, 128])
    gconv_p_o = dout("o_gconv_p", [2, 128, 12, 3])
    lru_p_o = dout("o_lru_p", [2, 128, 2])
    lconv_p_o = dout("o_lconv_p", [2, 128, 2, 3])
    fconv_p_o = dout("o_fconv_p", [2, 128, NFC, 2])
    hg_s_o = dout("o_hg_s", [2, 128, NSQ, 2, 64])
    gdn_s_o = dout("o_gdn_s", [2, NSQ, 4, 128, 128])
    gconv_s_o = dout("o_gconv_s", [2, 128, 12, NSQ, 3])
    lru_s_o = dout("o_lru_s", [2, 128, 2, NSQ])
    lconv_s_o = dout("o_lconv_s", [2, 128, 2, NSQ, 3])
    fconv_s_o = dout("o_fconv_s", [2, 128, NFC, NSQ, 2])

    def sb(name, shape, dt=F32):
        return nc.alloc_sbuf_tensor(name, list(shape), dt)

    x = sb("x_res", [128, 8, NT])
    hb = sb("h_bf", [128, 8, NT], BF16)
    warena = sb("warena", [128, 2 * SLOT], BF16)
    ident = sb("ident", [128, 128])
    ones_f = sb("ones_f", [128, 128])
    nones_f = sb("nones_f", [64, 64])
    ones_b = sb("ones_b", [128, 128], BF16)
    tri = sb("tri", [64, 64])
    nega = sb("nega", [64, 64])
    negt = sb("negt", [64, 64])
    rm32 = sb("rm32", [128, 256])
    rm4 = sb("rm4", [128, 64])
    mods = sb("mods", [128, 2, 48, 17])
    b_adaT = sb("b_adaT_s", [128, 2, 48])
    nmixT = sb("nmixT_s", [128, 2, 8])
    nffnT = sb("nffnT_s", [128, 2, 8])
    fnormT = sb("fnormT_s", [128, 8])
    coefA = sb("coefA", [128, 8, 17])
    hlbT = sb("hlbT_s", [128, 2, 2])
    lbT = sb("lbT", [128, 2, 2])
    omlT = sb("omlT", [128, 2, 2])
    nomlT = sb("nomlT", [128, 2, 2])
    hgnT = sb("hgnT_s", [64, 2])
    gcwT = sb("gcwT_s", [128, 2, 12, 4])
    negA = sb("negA", [64, 2, 4])
    dtb = sb("dtb", [64, 2, 4])
    gnT = sb("gnT_s", [128, 2])
    lcwT = sb("lcwT_s", [128, 2, 2, 4])
    lcbT = sb("lcbT_s", [128, 2, 2])
    lbaT = sb("lbaT_s", [128, 2, 2])
    lbxT = sb("lbxT_s", [128, 2, 2])
    nl8 = sb("nl8", [128, 2, 2])
    wabd = sb("wabd", [128, 2, 2, 128])
    wxbd = sb("wxbd", [128, 2, 2, 128])
    fcwT = sb("fcwT_s", [128, 2, NFC, 3])
    fcbT = sb("fcbT_s", [128, 2, NFC])
    S_hg = sb("S_hg", [128, 2, 64])
    S_hgB = sb("S_hgB", [128, 2, 64])
    hg_cur = [0]
    S_gd = sb("S_gd", [128, 2, 128])
    h_lru = sb("h_lru", [128, 2])
    fcarry = sb("fcarry", [128, NFC, 2])

    W = Arena(nc, "work", 36 * 1024)
    W2 = Arena(nc, "w2", SLOT * 2, tensor=warena, base=SLOT * 2)
    pb = [nc.alloc_psum_tensor("pb%d" % i, [128, 512], F32) for i in range(8)]

    slot_ctr = [0]

    def next_slot(force=None):
        if force is not None:
            slot_ctr[0] = force + 1
            return warena[:, force * SLOT:(force + 1) * SLOT]
        s = slot_ctr[0] % 2
        slot_ctr[0] += 1
        return warena[:, s * SLOT:(s + 1) * SLOT]

    def seq_tiles(TT, sgrp=NSQ):
        tl = [dict(t0=t0, TT=TT, nseq=1, T=TT, samp=False, first=(t0 == 0), last=(t0 + TT == TP), s0=0)
              for t0 in range(0, TP, TT)]
        for s0 in range(0, NSQ, sgrp):
            tl.append(dict(t0=TP + s0 * TSQ, TT=sgrp * TSQ, nseq=sgrp, T=TSQ, samp=True, first=False,
                           last=False, s0=s0))
        return tl

    P.memset(ident[:], 0.0)
    P.add("pool", lambda e: e.affine_select(out=ident[:], in_=ident[:], pattern=[[-1, 128]],
                                            compare_op=ALU.not_equal, fill=1.0, base=0,
                                            channel_multiplier=1), [ident[:]], [ident[:]])
    P.memset(ones_f[:], 1.0)
    P.memset(nones_f[:], -1.0)
    P.memset(ones_b[:], 1.0)
    P.memset(tri[:], 1.0)
    P.add("pool", lambda e: e.affine_select(out=tri[:], in_=tri[:], pattern=[[1, 64]],
                                            compare_op=ALU.is_ge, fill=0.0, base=0,
                                            channel_multiplier=-1), [tri[:]], [tri[:]])
    P.memset(nega[:], 0.0)
    P.add("pool", lambda e: e.affine_select(out=nega[:], in_=nega[:], pattern=[[-1, 64]],
                                            compare_op=ALU.is_gt, fill=-30000.0, base=0,
                                            channel_multiplier=1), [nega[:]], [nega[:]])
    P.memset(negt[:], 0.0)
    P.add("pool", lambda e: e.affine_select(out=negt[:], in_=negt[:], pattern=[[1, 64]],
                                            compare_op=ALU.is_ge, fill=-30000.0, base=0,
                                            channel_multiplier=-1), [negt[:]], [negt[:]])
    P.memset(rm32[:], 1.0)
    P.memset(rm32[:].rearrange("p (c t) -> p c t", t=32)[:, :, 0:1], 0.0)
    P.memset(rm4[:], 1.0)
    P.memset(rm4[:].rearrange("p (c t) -> p c t", t=4)[:, :, 0:1], 0.0)

    P.dma(x[:], xT_d.rearrange("(k p) t -> p k t", p=128))
    P.dma(b_adaT[:], b_adaT_d.rearrange("l p c -> p l c"))
    P.dma(nmixT[:], nmixT_d.rearrange("l p c -> p l c"))
    P.dma(nffnT[:], nffnT_d.rearrange("l p c -> p l c"))
    P.dma(fnormT[:], fnormT_d)
    P.dma(hlbT[:], hlbT_d.rearrange("l p c -> p l c"))
    P.dma(hgnT[:], hgnT_d)
    P.dma(gcwT[:], gcwT_d.rearrange("l p c j -> p l c j"))
    for l in range(2):
        P.dma(negA[:, l, :], galog_d[l:l + 1, :].partition_broadcast(64))
        P.dma(dtb[:, l, :], gdtb_d[l:l + 1, :].partition_broadcast(64))
    P.dma(gnT[:], gnT_d)
    P.dma(lcwT[:], lcwT_d.rearrange("l p c j -> p l c j"))
    P.dma(lcbT[:], lcbT_d.rearrange("l p c -> p l c"))
    P.dma(lbaT[:], lbaT_d.rearrange("l p c -> p l c"))
    P.dma(lbxT[:], lbxT_d.rearrange("l p c -> p l c"))
    P.dma(nl8[:], llamT_d.rearrange("l p c -> p l c"))
    P.dma(fcwT[:], fcwT_d.rearrange("l p c j -> p l c j"))
    P.dma(fcbT[:], fcbT_d.rearrange("l p c -> p l c"))
    P.memset(wabd[:], 0.0)
    P.memset(wxbd[:], 0.0)
    for l in range(2):
        for n in range(4):
            ch, j = n // 2, n % 2
            P.dma(wabd[64 * j:64 * j + 64, l, ch, 64 * j:64 * j + 64], lwa_d[l, n])
            P.dma(wxbd[64 * j:64 * j + 64, l, ch, 64 * j:64 * j + 64], lwx_d[l, n])
    P.act(negA[:], negA[:], AF.Exp)
    P.ts(negA[:], negA[:], -1.0, ALU.mult)
    P.act(nl8[:], nl8[:], AF.Exp, scale=-1.0)
    P.act(nl8[:], nl8[:], AF.Ln, bias=1.0)
    P.ts(nl8[:], nl8[:], -8.0, ALU.mult)
    P.memset(lbT[:, 0, :], 0.0)
    P.tt(lbT[:, 1, :], hlbT[:, 1, :], hlbT[:, 0, :], ALU.subtract)
    P.act(lbT[:, 1, :], lbT[:, 1, :], AF.Sigmoid)
    P.ts(omlT[:], lbT[:], -1.0, ALU.mult, 1.0, ALU.add)
    P.ts(nomlT[:], omlT[:], -1.0, ALU.mult)

    chk('c')
    W.reset()
    cT = W.alloc([128, 8, 17])
    scT = W.alloc([128, 8, 17], BF16)
    P.dma(cT, cT_d.rearrange("(k p) s -> p k s", p=128))
    P.act(scT, cT, AF.Silu)
    gi = 0
    for l in range(2):
        for g in range(12):
            sl = next_slot()
            sub = sl[:, 0:4096].rearrange("p (k n) -> p k n", k=8)
            P.dma(sub, w_ada_d[l].rearrange("(k p) n -> p k n", p=128)[:, :, 512 * g:512 * g + 512],
                  eng="pool")
            bank = pb[gi % 2]
            gi += 1
            for mc in range(4):
                for k in range(8):
                    P.mm(bank[:, mc * 17:mc * 17 + 17], sub[:, k, mc * 128:mc * 128 + 128],
                         scT[:, k, :], start=(k == 0), stop=(k == 7))
            for mc in range(4):
                chunk = g * 4 + mc
                plus = 1.0 if (8 <= chunk < 16 or 32 <= chunk < 40) else 0.0
                P.ts(mods[:, l, chunk, :], bank[:, mc * 17:mc * 17 + 17],
                     b_adaT[:, l, chunk:chunk + 1], ALU.add, plus, ALU.add)

    chk('m')
    def modnorm(l, normT, sc_off, sh_off):
        P.tt(coefA[:], mods[:, l, sc_off:sc_off + 8, :],
             bc(normT[:, l, :].unsqueeze(2), [128, 8, 17]), ALU.mult)
        for tl in seq_tiles(256):
            t0, TT = tl["t0"], tl["TT"]
            W.reset()
            sq = W.alloc([128, 8, TT], BF16)
            tmp = W.alloc([128, 8, TT])
            rs = W.alloc([128, TT])
            P.act(sq, x[:, :, t0:t0 + TT], AF.Square)
            ps = pb[0][:, 0:TT]
            for k in range(8):
                P.mm(ps, ones_b[:], sq[:, k, :], start=(k == 0), stop=(k == 7))
            P.act(rs, ps, AF.Sqrt, bias=EPS, scale=1.0 / D)
            P.recip(rs, rs)
            P.tt(tmp, x[:, :, t0:t0 + TT], bc(rs.unsqueeze(1), [128, 8, TT]), ALU.mult)
            if not tl["samp"]:
                for k in range(8):
                    P.ts(hb[:, k, t0:t0 + TT], tmp[:, k, :], coefA[:, k, 0:1], ALU.mult,
                         mods[:, l, sh_off + k, 0:1], ALU.add)
            else:
                tv = tmp.rearrange("p k (s t) -> p k s t", t=TSQ)
                P.tt(tv, tv, bc(coefA[:, :, 1:17].unsqueeze(3), [128, 8, NSQ, TSQ]), ALU.mult)
                P.tt(hb[:, :, t0:t0 + TT].rearrange("p k (s t) -> p k s t", t=TSQ), tv,
                     bc(mods[:, l, sh_off:sh_off + 8, 1:17].unsqueeze(3), [128, 8, NSQ, TSQ]),
                     ALU.add)

    def x_update(l, g_off, m, ps, tl):
        t0, TT = tl["t0"], tl["TT"]
        xs = x[:, m, t0:t0 + TT]
        if not tl["samp"]:
            if X_tmp[0] is not None:
                tmp = X_tmp[0][X_tmp[1] % 2][:, 0:TT]
                X_tmp[1] += 1
                P.act(tmp, ps, AF.Identity, scale=mods[:, l, g_off + m, 0:1])
                P.tt(xs, xs, tmp, ALU.add, eng="pool")
            else:
                P.stt(xs, ps, mods[:, l, g_off + m, 0:1], xs, ALU.mult, ALU.add)
        else:
            s0, ns = tl["s0"], tl["nseq"]
            tmp = W_x[0][:, 0:TT]
            tv = tmp.rearrange("p (s t) -> p s t", t=TSQ)
            P.tt(tv, ps.rearrange("p (s t) -> p s t", t=TSQ),
                 bc(mods[:, l, g_off + m, 1 + s0:1 + s0 + ns].unsqueeze(2), [128, ns, TSQ]), ALU.mult)
            P.tt(xs, xs, tmp, ALU.add)

    W_x = [None]
    X_tmp = [None, 0]

    def proj(ps, wv, c0, ncol, t0, TT):
        for k in range(8):
            P.mm(ps, wv[:, k, c0:c0 + ncol], hb[:, k, t0:t0 + TT], start=(k == 0), stop=(k == 7))

    for l in range(2):
        modnorm(l, nmixT, 8, 0)

        chk('n%d' % l)
        sl = next_slot(0)
        win = sl[:, 0:8192].rearrange("p (k n) -> p k n", k=8)
        wout = sl[0:64, 8192:12288].rearrange("p (h n) -> p h n", h=4)
        P.dma(win, w_in_d[l].rearrange("(k p) n -> p k n", p=128)[:, :, 0:1024], eng="pool")
        P.dma(wout, w_out_d[l][0:256, :].rearrange("(h e) n -> e h n", e=64), eng="pool")
        for tl in seq_tiles(256, 8):
            t0, TT, samp = tl["t0"], tl["TT"], tl["samp"]
            s0, nsq = tl["s0"], tl["nseq"]
            C = 4 if samp else 32
            NC = TT // C
            W.reset()
            W2.reset()
            W_x[0] = W.alloc([128, NTS])
            late_off = W.off
            q = W.alloc([128, 2, TT])
            sg = W.alloc([128, 2, TT])
            kk = W.alloc([128, 2, TT])
            G = W.alloc([128, 2, TT])
            eG = W.alloc([128, 2, TT])
            vT = W.alloc([128, 2, TT])
            hzs = W.alloc([64, 4, TT])
            qgm = W.alloc([128, 2, 2, TT])
            kgl = W.alloc([128, 2, TT])
            eGl = W.alloc([128, 2, NC])
            AT = W.alloc([32, NC * 4, C])
            ohg = W.alloc([64, 4, TT])
            Ssm = W.alloc([128, nsq, 2, 64]) if samp else None
            kglm = W2.alloc([32, NC, 2, 2, 128])
            vtm = W2.alloc([32, NC, 2, 128])
            bi = 0
            for c in range(2):
                ps = pb[bi % 4][:, 0:TT]; bi += 1
                proj(ps, win, c * 128, 128, t0, TT)
                P.act(q[:, c, :], ps, AF.Silu)
            for c in range(2):
                ps = pb[bi % 4][:, 0:TT]; bi += 1
                proj(ps, win, 256 + c * 128, 128, t0, TT)
                P.act(sg[:, c, :], ps, AF.Sigmoid)
            for c in range(2):
                ps = pb[bi % 4][:, 0:TT]; bi += 1
                proj(ps, win, 512 + c * 128, 128, t0, TT)
                P.copy(vT[:, c, :], ps, eng="act")
            for h in range(4):
                ps = pb[bi % 4][0:64, 0:TT]; bi += 1
                proj(ps, win, 768 + h * 64, 64, t0, TT)
                P.act(hzs[:, h, :], ps, AF.Silu)
            for c in range(2):
                P.ts(kk[:, c, :], sg[:, c, :], nomlT[:, l, c:c + 1], ALU.mult,
                     omlT[:, l, c:c + 1], ALU.add)
                P.ts(sg[:, c, :], sg[:, c, :], omlT[:, l, c:c + 1], ALU.mult,
                     lbT[:, l, c:c + 1], ALU.add)
            P.act(sg, sg, AF.Ln)
            rm = rm4[:, 0:TT] if samp else rm32[:, 0:TT]
            for c in range(2):
                P.scan(G[:, c, :], rm, sg[:, c, :], 0.0)
            P.act(eG, G, AF.Exp)
            P.memset(qgm, 0.0)
            P.tt(qgm[0:64, :, 0, :], q[0:64], eG[0:64], ALU.mult)
            P.tt(qgm[64:128, :, 1, :], q[64:128], eG[64:128], ALU.mult)
            P.act(eG, G, AF.Exp, scale=-1.0)
            P.tt(kk, kk, eG, ALU.mult)
            Gv = G.rearrange("p a (c t) -> p a c t", t=C)
            P.act(eGl, Gv[:, :, :, C - 1], AF.Exp)
            P.tt(kgl.rearrange("p a (c t) -> p a c t", t=C), kk.rearrange("p a (c t) -> p a c t", t=C),
                 bc(eGl.unsqueeze(3), [128, 2, NC, C]), ALU.mult)
            psA = pb[4]
            psA2 = pb[5]
            per_bank = 512 // C
            for c in range(NC):
                for h in range(4):
                    pr = c * 4 + h
                    bank = psA if pr < per_bank else psA2
                    o = (pr % per_bank) * C
                    P.mm(bank[0:C, o:o + C], kk[:, h // 2, c * C:(c + 1) * C],
                         qgm[:, h // 2, h % 2, c * C:(c + 1) * C])
            npr = NC * 4
            n0 = min(npr, per_bank)
            P.tt(AT[0:C, 0:n0, :], psA[0:C, 0:n0 * C].rearrange("p (a t) -> p a t", t=C),
                 bc(tri[0:C, 0:C].unsqueeze(1), [C, n0, C]), ALU.mult)
            if npr > per_bank:
                n1 = npr - per_bank
                P.tt(AT[0:C, n0:npr, :], psA2[0:C, 0:n1 * C].rearrange("p (a t) -> p a t", t=C),
                     bc(tri[0:C, 0:C].unsqueeze(1), [C, n1, C]), ALU.mult)
            P.memset(kglm, 0.0)
            for c0 in range(0, NC, 2):
                bk = pb[6]
                bv = pb[7]
                for ci in range(2):
                    c = c0 + ci
                    for p2 in range(2):
                        o = (ci * 2 + p2) * 128
                        P.tr(bk[0:C, o:o + 128], kgl[:, p2, c * C:(c + 1) * C], ident[:])
                        P.tr(bv[0:C, o:o + 128], vT[:, p2, c * C:(c + 1) * C], ident[:])
                bkv = bk[0:C, :].rearrange("p (c a e) -> p c a e", c=2, a=2)
                P.copy(kglm[0:C, c0:c0 + 2, :, 0, 0:64], bkv[:, :, :, 0:64], eng="act")
                P.copy(kglm[0:C, c0:c0 + 2, :, 1, 64:128], bkv[:, :, :, 64:128], eng="act")
                P.copy(vtm[0:C, c0:c0 + 2, :, :], bv[0:C, :].rearrange("p (c a e) -> p c a e", c=2, a=2))
            if samp:
                P.dma(Ssm, st_hg_d[l][:, s0:s0 + nsq])
            elif tl["first"]:
                hg_cur[0] = 0
                P.memset(S_hg[:], 0.0)
            for c in range(NC):
                if samp:
                    S = Sn = Ssm[:, c]
                else:
                    S = (S_hg, S_hgB)[hg_cur[0]]
                    Sn = (S_hg, S_hgB)[1 - hg_cur[0]]
                    hg_cur[0] = 1 - hg_cur[0]
                po = pb[(c % 2)][0:64, 0:4 * C]
                for h in range(4):
                    p2, j = h // 2, h % 2
                    P.mm(po[:, h * C:(h + 1) * C], S[:, p2, :], qgm[:, p2, j, c * C:(c + 1) * C],
                         start=True, stop=False)
                    P.mm(po[:, h * C:(h + 1) * C], vtm[0:C, c, p2, 64 * j:64 * j + 64],
                         AT[0:C, c * 4 + h, :], start=False, stop=True)
                P.copy(ohg[:, :, c * C:(c + 1) * C], po.rearrange("p (h t) -> p h t", h=4), eng="act")
                for p2 in range(2):
                    pS = pb[2 + p2][:, 0:64]
                    P.mm(pS, kglm[0:C, c, p2, 0, :], vtm[0:C, c, p2, 0:64], start=True, stop=False)
                    P.mm(pS, kglm[0:C, c, p2, 1, :], vtm[0:C, c, p2, 64:128], start=False, stop=True)
                    P.stt(Sn[:, p2, :], S[:, p2, :], eGl[:, p2, c:c + 1], pS, ALU.mult, ALU.add)
            if samp:
                P.dma(hg_s_o[l][:, s0:s0 + nsq], Ssm)
            elif tl["last"]:
                P.dma(hg_p_o[l], (S_hg, S_hgB)[hg_cur[0]][:])
            W.reset(late_off)
            ohb = W.alloc([64, 4, TT], BF16)
            sqo = W.alloc([64, 4, TT], BF16)
            rinv = W.alloc([64, 4, TT])
            P.act(sqo, ohg, AF.Square)
            for h in range(4):
                bank = pb[4 + h // 2]
                P.mm(bank[0:64, (h % 2) * TT:(h % 2) * TT + TT], ones_b[0:64, 0:64], sqo[:, h, :])
            for hp in range(2):
                P.act(rinv[:, 2 * hp:2 * hp + 2, :],
                      pb[4 + hp][0:64, 0:2 * TT].rearrange("p (h t) -> p h t", h=2),
                      AF.Sqrt, bias=EPS, scale=1.0 / 64)
            P.recip(rinv, rinv)
            P.tt(ohg, ohg, rinv, ALU.mult)
            P.stt(ohb, ohg, hgnT[:, l:l + 1], hzs, ALU.mult, ALU.mult)
            for m in range(8):
                ps = pb[m % 4][:, 0:TT]
                for h in range(4):
                    P.mm(ps, wout[:, h, m * 128:(m + 1) * 128], ohb[:, h, :],
                         start=(h == 0), stop=(h == 3))
                x_update(l, 16, m, ps, tl)

        chk('hg%d' % l)
        for hp in range(2):
            sl = next_slot(0)
            win = sl[:, 0:8 * 1028].rearrange("p (k n) -> p k n", k=8)
            wout = sl[:, 8 * 1028:8 * 1028 + 2048].rearrange("p (h n) -> p h n", h=2)
            wsrc = w_in_d[l].rearrange("(k p) n -> p k n", p=128)
            for gi_, base in enumerate((1024, 1536, 2048, 2560)):
                P.dma(win[:, :, gi_ * 256:(gi_ + 1) * 256],
                      wsrc[:, :, base + 256 * hp:base + 256 * hp + 256], eng="pool")
            P.dma(win[:, :, 1024:1026], wsrc[:, :, 3072 + 2 * hp:3072 + 2 * hp + 2], eng="pool")
            P.dma(win[:, :, 1026:1028], wsrc[:, :, 3076 + 2 * hp:3076 + 2 * hp + 2], eng="pool")
            P.dma(wout, w_out_d[l][256 + 256 * hp:512 + 256 * hp, :].rearrange("(h e) n -> e h n", e=128),
                  eng="pool")
            chid = [2 * hp, 2 * hp + 1, 4 + 2 * hp, 5 + 2 * hp, 8 + 2 * hp, 9 + 2 * hp]
            carry = None
            W.reset()
            gst = W.alloc([128, 6, NSQ, 3])
            gso = W.alloc([128, 6, NSQ, 3])
            for i6 in range(6):
                P.dma(gst[:, i6], st_gconv_d[l][:, chid[i6]])
            base_off = W.off
            for tl in seq_tiles(256, 8):
                t0, TT, samp, nseq, T = tl["t0"], tl["TT"], tl["samp"], tl["nseq"], tl["T"]
                s0 = tl["s0"]
                C = 4 if samp else 64
                NC = TT // C
                NPR = NC * 2
                nlev = 1 if samp else 5
                W.reset(base_off)
                W2.reset()
                W_x[0] = W.alloc([128, NTS])
                carry_buf = W.alloc([128, 6, 3])
                gzs = W.alloc([128, 2, TT])
                gab = W.alloc([64, NPR, 2])
                gT = W.alloc([64, NPR])
                bT = W.alloc([64, NPR])
                nbT = W.alloc([64, NPR])
                gcs = W.alloc([64, NPR])
                egl = W.alloc([128, NPR])
                bw = W.alloc([64, NPR])
                ekl = W.alloc([64, NPR])
                Pm = [W.alloc([64, NPR, C]), None]
                PTm = [W.alloc([128, NPR, C]), None]
                dd_off = W.off
                DA = W.alloc([64, NPR, C])
                DT = W.alloc([64, NPR, C])
                W.reset(dd_off)
                Pm[1] = W.alloc([64, NPR, C])
                PTm[1] = W.alloc([64, NPR, C])
                late_off = W.off
                ext = W.alloc([128, 6, nseq, 3 + T])
                cv = W.alloc([128, 6, TT])
                sq = W.alloc([128, 4, TT])
                Sb = [W.alloc([128, 2, 128]) for _ in range(3)] if samp else None
                ru = W2.alloc([64, NPR, 128])
                rw = W2.alloc([64, NPR, 128])
                kgl = W2.alloc([64, NPR, 128])
                Wsm = W if samp else W2
                Lg = Wsm.alloc([64, NPR, C])
                attT = Wsm.alloc([64, NPR, C])
                Xm = [Wsm.alloc([64, NPR, C]) for _ in range(2)]
                nwT = Wsm.alloc([128, NPR, C])
                qg = Wsm.alloc([128, NPR, C])
                extv = ext
                bi = 0
                for i6 in range(6):
                    ps = pb[bi % 4][:, 0:TT]; bi += 1
                    proj(ps, win, i6 * 128, 128, t0, TT)
                    P.copy(ext[:, i6, :, 3:3 + T], ps.rearrange("p (s t) -> p s t", t=T), eng="act")
                for c in range(2):
                    ps = pb[bi % 4][:, 0:TT]; bi += 1
                    proj(ps, win, 768 + c * 128, 128, t0, TT)
                    P.act(gzs[:, c, :], ps, AF.Silu)
                if l == 0 and hp == 0 and tl['first']:
                    chk('g1')
                psg = pb[4]
                for c in range(NC):
                    for k in range(8):
                        P.mm(psg[0:C, c * 4:c * 4 + 4], hb[:, k, t0 + c * C:t0 + (c + 1) * C],
                             win[:, k, 1024:1028], start=(k == 0), stop=(k == 7))
                psgv = psg[0:C, 0:NC * 4].rearrange("p (c f) -> p c f", f=4)
                gabv = gab[0:C].rearrange("p (c h) f -> p c h f", h=2)
                P.copy(gabv[:, :, :, 0], psgv[:, :, 0:2])
                P.copy(gabv[:, :, :, 1], psgv[:, :, 2:4])
                if l == 0 and hp == 0 and tl['first']:
                    chk('g1b')
                if samp:
                    P.copy(ext[:, :, :, 0:3], gst[:, :, s0:s0 + nseq, :])
                elif tl["first"]:
                    P.memset(ext[:, :, :, 0:3], 0.0)
                else:
                    P.copy(ext[:, :, 0, 0:3], carry)
                cvv = cv.rearrange("p c (s t) -> p c s t", t=T)
                for i6 in range(6):
                    P.act(cvv[:, i6], ext[:, i6, :, 0:T], AF.Identity, scale=gcwT[:, l, chid[i6], 0:1])
                    for j in range(1, 4):
                        P.stt(cvv[:, i6], ext[:, i6, :, j:j + T], gcwT[:, l, chid[i6], j:j + 1],
                              cvv[:, i6], ALU.mult, ALU.add)
                if samp:
                    P.copy(gso[:, :, s0:s0 + nseq, :], ext[:, :, :, T:T + 3])
                else:
                    P.copy(carry_buf, ext[:, :, 0, T:T + 3])
                    if tl["last"]:
                        for i6 in range(6):
                            P.dma(gconv_p_o[l][:, chid[i6], :], carry_buf[:, i6, :])
                P.act(cv, cv, AF.Silu)
                if l == 0 and hp == 0 and tl['first']:
                    chk('g2')
                P.act(sq, cv[:, 0:4, :], AF.Square)
                for i4 in range(4):
                    bank = pb[5 + i4 // 2]
                    P.mm(bank[:, (i4 % 2) * TT:(i4 % 2) * TT + TT], ones_f[:], sq[:, i4, :])
                for i2 in range(2):
                    P.act(sq[:, 2 * i2:2 * i2 + 2, :],
                          pb[5 + i2][:, 0:2 * TT].rearrange("p (h t) -> p h t", h=2),
                          AF.Sqrt, bias=EPS)
                P.recip(sq, sq)
                P.stt(cv[:, 0:2, :], cv[:, 0:2, :], 128.0 ** -0.5, sq[:, 0:2, :], ALU.mult, ALU.mult)
                P.tt(cv[:, 2:4, :], cv[:, 2:4, :], sq[:, 2:4, :], ALU.mult)
                qT = cv[:, 0:2, :]
                kT = cv[:, 2:4, :]
                vT = cv[:, 4:6, :]
                if l == 0 and hp == 0 and tl['first']:
                    chk('g3')
                gabr = gab[0:C].rearrange("p (c h) f -> p c h f", h=2)
                gTv = gT[0:C].rearrange("p (c h) -> p c h", h=2)
                P.tt(gTv, gabr[:, :, :, 0],
                     bc(dtb[0:C, l, 2 * hp:2 * hp + 2].unsqueeze(1), [C, NC, 2]), ALU.add)
                P.act(gT[0:C], gT[0:C], AF.Exp)
                P.act(gT[0:C], gT[0:C], AF.Ln, bias=1.0)
                P.tt(gTv, gTv, bc(negA[0:C, l, 2 * hp:2 * hp + 2].unsqueeze(1), [C, NC, 2]), ALU.mult)
                P.act(bT[0:C].rearrange("p (c h) -> p c h", h=2), gabr[:, :, :, 1], AF.Sigmoid)
                P.ts(nbT[0:C], bT[0:C], -1.0, ALU.mult)
                P.tt(Lg[0:C], bc(tri[0:C, 0:C].unsqueeze(1), [C, NPR, C]),
                     bc(gT[0:C].unsqueeze(2), [C, NPR, C]), ALU.mult)
                psc = pb[4]
                P.mm(psc[0:C, 64:64 + NPR], tri[0:C, 0:C], gT[0:C, :])
                P.mm(psc[:, 128:128 + NPR], ones_f[0:C, :], gT[0:C, :])
                P.copy(gcs[0:C], psc[0:C, 64:64 + NPR], eng="act")
                P.act(egl, psc[:, 128:128 + NPR], AF.Exp)
                P.act(bw[0:C], gcs[0:C], AF.Exp)
                P.tt(bw[0:C], bw[0:C], bT[0:C], ALU.mult)
                P.tt(ekl[0:C], psc[0:C, 128:128 + NPR], gcs[0:C], ALU.subtract)
                P.act(ekl[0:C], ekl[0:C], AF.Exp)
                if l == 0 and hp == 0 and tl['first']:
                    chk('g4')
                pKK, pKQ, pDA, pDT = pb[0], pb[1], pb[2], pb[3]
                for pr in range(NPR):
                    c, hh = pr // 2, pr % 2
                    cs = slice(c * C, (c + 1) * C)
                    o = pr * C
                    P.mm(pKK[0:C, o:o + C], kT[:, hh, cs], kT[:, hh, cs])
                    P.mm(pKQ[0:C, o:o + C], kT[:, hh, cs], qT[:, hh, cs])

                def pv(bank):
                    return bank[0:C, 0:NPR * C].rearrange("p (a t) -> p a t", t=C)
                P.memset(PTm[0], 0.0)
                P.mm(pDA[0:C, 0:NPR * C], ones_f[0:C, 0:C], Lg[0:C].rearrange("p a t -> p (a t)"))
                gcb = bc(gcs[0:C].unsqueeze(2), [C, NPR, C])
                P.tt(DA[0:C], gcb, pv(pDA), ALU.subtract)
                P.tt(DA[0:C], DA[0:C], bc(nega[0:C, 0:C].unsqueeze(1), [C, NPR, C]), ALU.add)
                P.tt(DT[0:C], pv(pDA), gcb, ALU.subtract)
                P.tt(DT[0:C], DT[0:C], bc(negt[0:C, 0:C].unsqueeze(1), [C, NPR, C]), ALU.add)
                P.act(DA[0:C], DA[0:C], AF.Exp)
                P.act(DT[0:C], DT[0:C], AF.Exp)
                P.tt(PTm[0][0:C], pv(pKK), DA[0:C], ALU.mult)
                P.tt(PTm[0][0:C], PTm[0][0:C], bc(nbT[0:C].unsqueeze(2), [C, NPR, C]), ALU.mult)
                P.tt(attT[0:C], pv(pKQ), DT[0:C], ALU.mult)
                if l == 0 and hp == 0 and tl['first']:
                    chk('g5')
                for p0 in range(0, NPR, 4):
                    pP = pb[5 + (p0 // 4) % 2]
                    n4 = min(4, NPR - p0)
                    for i4 in range(n4):
                        P.tr(pP[0:C, i4 * 128:(i4 + 1) * 128], PTm[0][:, p0 + i4, :], ident[:])
                    if l == 0 and hp == 0 and tl['first']:
                        chk('g5a')
                    pPv = pP[0:C, 0:n4 * 128].rearrange("p (a e) -> p a e", e=128)[:, :, 0:C]
                    P.copy(Pm[0][0:C, p0:p0 + n4, :], pPv, eng="act")
                    if l == 0 and hp == 0 and tl['first']:
                        chk('g5c')
                    P.tt(Xm[0][0:C, p0:p0 + n4, :], Pm[0][0:C, p0:p0 + n4, :],
                         bc(ident[0:C, 0:C].unsqueeze(1), [C, n4, C]), ALU.add)
                if l == 0 and hp == 0 and tl['first']:
                    chk('g5b')
                def kv_group(p0):
                    bk, bv = pb[0], pb[1]
                    n4 = min(4, NPR - p0)
                    for i4 in range(n4):
                        pr = p0 + i4
                        c, hh = pr // 2, pr % 2
                        cs = slice(c * C, (c + 1) * C)
                        P.tr(bk[0:C, i4 * 128:(i4 + 1) * 128], kT[:, hh, cs], ident[:])
                        P.tr(bv[0:C, i4 * 128:(i4 + 1) * 128], vT[:, hh, cs], ident[:])
                    bkv = bk[0:C, 0:n4 * 128].rearrange("p (a e) -> p a e", e=128)
                    bvv = bv[0:C, 0:n4 * 128].rearrange("p (a e) -> p a e", e=128)
                    P.tt(ru[0:C, p0:p0 + n4, :], bvv, bc(bT[0:C, p0:p0 + n4].unsqueeze(2), [C, n4, 128]),
                         ALU.mult)
                    P.tt(rw[0:C, p0:p0 + n4, :], bkv, bc(bw[0:C, p0:p0 + n4].unsqueeze(2), [C, n4, 128]),
                         ALU.mult)
                    P.tt(kgl[0:C, p0:p0 + n4, :], bkv, bc(ekl[0:C, p0:p0 + n4].unsqueeze(2), [C, n4, 128]),
                         ALU.mult)

                def e_block():
                    pE = pb[3]
                    P.mm(pE[:, 0:NPR * C], ones_f[0:C, :], Lg[0:C].rearrange("p a t -> p (a t)"))
                    P.act(qg, pE[:, 0:NPR * C].rearrange("p (a t) -> p a t", t=C), AF.Exp)
                    qgv = qg.rearrange("p (c h) t -> p h c t", h=2)
                    P.tt(qgv, qgv, qT.rearrange("p h (c t) -> p h c t", t=C), ALU.mult)

                pending = [(lambda p0=p0: kv_group(p0)) for p0 in range(0, NPR, 4)] + [e_block]
                cur = 0
                for lev in range(1, nlev + 1):
                    nxt = 1 - cur
                    last = (lev == nlev)
                    pA_, pB_, pC_ = pb[5], pb[6], pb[7]
                    if not last:
                        for pr in range(NPR):
                            P.mm(pA_[0:C, pr * C:(pr + 1) * C], PTm[cur][0:C, pr, :], Pm[cur][0:C, pr, :])
                    for pr in range(NPR):
                        P.mm(pB_[0:C, pr * C:(pr + 1) * C], Pm[cur][0:C, pr, :], PTm[cur][0:C, pr, :])
                    if not last:
                        P.copy(Pm[nxt][0:C], pv(pA_), eng="act")
                    P.copy(PTm[nxt][0:C], pv(pB_))
                    if pending:
                        pending.pop(0)()
                    for pr in range(NPR):
                        P.mm(pC_[0:C, pr * C:(pr + 1) * C], PTm[nxt][0:C, pr, :], Xm[cur][0:C, pr, :],
                             start=True, stop=False)
                        P.mm(pC_[0:C, pr * C:(pr + 1) * C], ident[0:C, 0:C], Xm[cur][0:C, pr, :],
                             start=False, stop=True)
                    P.copy(Xm[nxt][0:C], pv(pC_), eng="act")
                    cur = nxt
                X = Xm[cur]
                if l == 0 and hp == 0 and tl['first']:
                    chk('g6')
                for th in pending:
                    th()
                pW = pb[2]
                for pr in range(NPR):
                    P.mm(pW[:, pr * C:(pr + 1) * C], rw[0:C, pr, :], X[0:C, pr, :])
                P.act(nwT, pW[:, 0:NPR * C].rearrange("p (a t) -> p a t", t=C), AF.Copy, scale=-1.0)
                if not samp:
                    W.reset(late_off)
                vnew = [W.alloc([64, 128]) for _ in range(4)]
                ogd = W.alloc([128, 2, TT])
                ogb = W.alloc([128, 2, TT], BF16)
                sqo = W.alloc([128, 2, TT], BF16)
                rinv = W.alloc([128, 2, TT])
                if (not samp) and tl["first"]:
                    P.memset(S_gd[:], 0.0)
                vi = 0
                for c in range(NC):
                    if samp:
                        S = Sb[c % 3]
                        P.dma(S, st_gdn_d[l, s0 + c, 2 * hp:2 * hp + 2].rearrange("h d e -> d h e"))
                    else:
                        S = S_gd
                    pO = pb[4 + (c % 2)]
                    vns = []
                    for hh in range(2):
                        pr = c * 2 + hh
                        pV = pb[6 + hh][0:C, 0:128]
                        P.mm(pV, X[0:C, pr, :], ru[0:C, pr, :], start=True, stop=False)
                        P.mm(pV, nwT[:, pr, :], S[:, hh, :], start=False, stop=True)
                    for hh in range(2):
                        pV = pb[6 + hh][0:C, 0:128]
                        vn = vnew[vi % 4]; vi += 1
                        vns.append(vn)
                        P.copy(vn[0:C, :], pV, eng="act")
                    for hh in range(2):
                        pr = c * 2 + hh
                        vn = vns[hh]
                        P.mm(pO[:, hh * C:(hh + 1) * C], S[:, hh, :], qg[:, pr, :], start=True, stop=False)
                        P.mm(pO[:, hh * C:(hh + 1) * C], vn[0:C, :], attT[0:C, pr, :], start=False, stop=True)
                    for hh in range(2):
                        pr = c * 2 + hh
                        pS = pb[6 + hh][:, 256:384]
                        P.mm(pS, kgl[0:C, pr, :], vns[hh][0:C, :])
                    for hh in range(2):
                        pr = c * 2 + hh
                        pS = pb[6 + hh][:, 256:384]
                        P.stt(S[:, hh, :], S[:, hh, :], egl[:, pr:pr + 1], pS, ALU.mult, ALU.add)
                    P.copy(ogd[:, :, c * C:(c + 1) * C],
                           pO[:, 0:2 * C].rearrange("p (h t) -> p h t", h=2), eng="act")
                    if samp:
                        P.dma(gdn_s_o[l, s0 + c, 2 * hp:2 * hp + 2].rearrange("h d e -> d h e"), S)
                if (not samp) and tl["last"]:
                    P.dma(gdn_p_o[l, 2 * hp:2 * hp + 2].rearrange("h d e -> d h e"), S_gd[:])
                carry = carry_buf
                if l == 0 and hp == 0 and tl['first']:
                    chk('g9')
                P.act(sqo, ogd, AF.Square)
                pm = pb[0]
                for hh in range(2):
                    P.mm(pm[:, hh * TT:(hh + 1) * TT], ones_b[:], sqo[:, hh, :])
                P.act(rinv, pm[:, 0:2 * TT].rearrange("p (h t) -> p h t", h=2), AF.Sqrt, bias=EPS,
                      scale=1.0 / 128)
                P.recip(rinv, rinv)
                P.tt(ogd, ogd, rinv, ALU.mult)
                P.stt(ogb, ogd, gnT[:, l:l + 1], gzs, ALU.mult, ALU.mult)
                for m in range(8):
                    ps = pb[1 + m % 3][:, 0:TT]
                    for hh in range(2):
                        P.mm(ps, wout[:, hh, m * 128:(m + 1) * 128], ogb[:, hh, :],
                             start=(hh == 0), stop=(hh == 1))
                    x_update(l, 16, m, ps, tl)
            for i6 in range(6):
                P.dma(gconv_s_o[l][:, chid[i6]], gso[:, i6])

        chk('gd%d' % l)
        sl = next_slot(0)
        win = sl[:, 0:4096].rearrange("p (k n) -> p k n", k=8)
        wout = sl[:, 4096:6144].rearrange("p (h n) -> p h n", h=2)
        P.dma(win, w_in_d[l].rearrange("(k p) n -> p k n", p=128)[:, :, 3080:3592], eng="pool")
        P.dma(wout, w_out_d[l][768:1024, :].rearrange("(h e) n -> e h n", e=128), eng="pool")
        carry = None
        W.reset()
        lst = W.alloc([128, 2, NSQ, 3])
        lso = W.alloc([128, 2, NSQ, 3])
        P.dma(lst, st_lconv_d[l])
        base_off = W.off
        for tl in seq_tiles(512):
            t0, TT, samp, nseq, T = tl["t0"], tl["TT"], tl["samp"], tl["nseq"], tl["T"]
            W.reset(base_off)
            W2.reset()
            W_x[0] = W.alloc([128, NTS])
            carry_buf = W.alloc([128, 2, 3])
            ext = W.alloc([128, 2, nseq, 3 + T])
            xc = W.alloc([128, 2, TT])
            glz = W.alloc([128, 2, TT])
            rr = W2.alloc([128, 2, TT])
            ii = W2.alloc([128, 2, TT])
            aa = W2.alloc([128, 2, TT])
            bb = W.alloc([128, 2, TT])
            hh_ = W.alloc([128, 2, TT])
            olb = W.alloc([128, 2, TT], BF16)
            h0s = W.alloc([128, 2, NSQ]) if samp else None
            h1s = W.alloc([128, 2, NSQ]) if samp else None
            for c in range(2):
                ps = pb[c][:, 0:TT]
                proj(ps, win, c * 128, 128, t0, TT)
                P.copy(ext[:, c, :, 3:3 + T], ps.rearrange("p (s t) -> p s t", t=T), eng="act")
            for c in range(2):
                ps = pb[2 + c][:, 0:TT]
                proj(ps, win, 256 + c * 128, 128, t0, TT)
                P.act(glz[:, c, :], ps, AF.Gelu_apprx_tanh)
            if samp:
                P.copy(ext[:, :, :, 0:3], lst)
                P.dma(h0s, st_lru_d[l])
            elif tl["first"]:
                P.memset(ext[:, :, :, 0:3], 0.0)
                P.memset(h_lru[:], 0.0)
            else:
                P.copy(ext[:, :, 0, 0:3], carry)
            xcv = xc.rearrange("p c (s t) -> p c s t", t=T)
            for c in range(2):
                P.ts(xcv[:, c], ext[:, c, :, 0:T], lcwT[:, l, c, 0:1], ALU.mult,
                     lcbT[:, l, c:c + 1], ALU.add)
                for j in range(1, 4):
                    P.stt(xcv[:, c], ext[:, c, :, j:j + T], lcwT[:, l, c, j:j + 1], xcv[:, c],
                          ALU.mult, ALU.add)
            if samp:
                P.copy(lso, ext[:, :, :, T:T + 3])
                P.dma(lconv_s_o[l], lso)
            else:
                P.copy(carry_buf, ext[:, :, 0, T:T + 3])
                if tl["last"]:
                    P.dma(lconv_p_o[l], carry_buf)
            carry = carry_buf
            for c in range(2):
                pr_ = pb[4 + c][:, 0:TT]
                pi_ = pb[6 + c][:, 0:TT]
                P.mm(pr_, wabd[:, l, c, :], xc[:, c, :])
                P.mm(pi_, wxbd[:, l, c, :], xc[:, c, :])
                P.act(rr[:, c, :], pr_, AF.Sigmoid, bias=lbaT[:, l, c:c + 1])
                P.act(ii[:, c, :], pi_, AF.Sigmoid, bias=lbxT[:, l, c:c + 1])
                P.act(aa[:, c, :], rr[:, c, :], AF.Exp, scale=nl8[:, l, c:c + 1])
            P.tt(bb, aa, aa, ALU.mult)
            P.ts(bb, bb, -1.0, ALU.mult, 1.0, ALU.add)
            P.act(bb, bb, AF.Sqrt)
            P.tt(ii, ii, xc, ALU.mult)
            P.tt(bb, bb, ii, ALU.mult)
            for c in range(2):
                if not samp:
                    P.scan(hh_[:, c, :], aa[:, c, :], bb[:, c, :], h_lru[:, c:c + 1])
                    P.copy(h_lru[:, c:c + 1], hh_[:, c, TT - 1:TT])
                else:
                    for s in range(NSQ):
                        P.scan(hh_[:, c, s * T:(s + 1) * T], aa[:, c, s * T:(s + 1) * T],
                               bb[:, c, s * T:(s + 1) * T], h0s[:, c, s:s + 1])
            if samp:
                P.copy(h1s, hh_.rearrange("p c (s t) -> p c s t", t=T)[:, :, :, T - 1])
                P.dma(lru_s_o[l], h1s)
            elif tl["last"]:
                P.dma(lru_p_o[l], h_lru[:])
            P.tt(olb, hh_, glz, ALU.mult)
            for m in range(8):
                ps = pb[m % 4][:, 0:TT]
                for c in range(2):
                    P.mm(ps, wout[:, c, m * 128:(m + 1) * 128], olb[:, c, :],
                         start=(c == 0), stop=(c == 1))
                x_update(l, 16, m, ps, tl)

        chk('lru%d' % l)
        modnorm(l, nffnT, 32, 24)
        parts = [(0, 4), (4, 8), (8, 12), (12, 16), (16, 19), (19, 22)]
        W.reset()
        W_x[0] = W.alloc([128, NTS])
        X_tmp[0] = [W.alloc([128, 512]) for _ in range(2)]
        fst = W.alloc([128, NFC, NSQ, 2])
        fout = W.alloc([128, NFC, NSQ, 2])
        P.dma(fst, st_fconv_d[l])
        base_off = W.off

        def load_part(f0, f1):
            nf = f1 - f0
            sl = next_slot()
            wg = sl[:, 0:8 * nf * 128].rearrange("p (k n) -> p k n", k=8)
            wu = sl[:, 4096:4096 + 8 * nf * 128].rearrange("p (k n) -> p k n", k=8)
            wd = sl[:, 8192:8192 + nf * 1024].rearrange("p (f n) -> p f n", f=nf)
            P.dma(wg, w_gate_d[l].rearrange("(k p) n -> p k n", p=128)[:, :, f0 * 128:f1 * 128], eng="pool")
            P.dma(wu, w_up_d[l].rearrange("(k p) n -> p k n", p=128)[:, :, f0 * 128:f1 * 128], eng="pool")
            P.dma(wd, w_down_d[l][f0 * 128:f1 * 128, :].rearrange("(f p) n -> p f n", p=128), eng="pool")
            return wg, wu, wd

        nxt_w = load_part(*parts[0])
        for pi, (f0, f1) in enumerate(parts):
            nf = f1 - f0
            wg, wu, wd = nxt_w
            if pi + 1 < len(parts):
                nxt_w = load_part(*parts[pi + 1])
            for tl in seq_tiles(512):
                t0, TT, samp, nseq, T = tl["t0"], tl["TT"], tl["samp"], tl["nseq"], tl["T"]
                W.reset(base_off)
                gext = [W.alloc([128, nseq, 2 + T]) for _ in range(2)]
                acc = [W.alloc([128, nseq, T]) for _ in range(2)]
                mt = W.alloc([128, nf, TT], BF16)
                for fi in range(nf):
                    fg = f0 + fi
                    psG = pb[(fi % 2)][:, 0:TT]
                    psU = pb[2 + (fi % 2)][:, 0:TT]
                    for k in range(8):
                        P.mm(psG, wg[:, k, fi * 128:(fi + 1) * 128], hb[:, k, t0:t0 + TT],
                             start=(k == 0), stop=(k == 7))
                    for k in range(8):
                        P.mm(psU, wu[:, k, fi * 128:(fi + 1) * 128], hb[:, k, t0:t0 + TT],
                             start=(k == 0), stop=(k == 7))
                    ge = gext[fi % 2]
                    ac = acc[fi % 2]
                    P.copy(ge[:, :, 2:2 + T], psG.rearrange("p (s t) -> p s t", t=T), eng="act")
                    if samp:
                        P.copy(ge[:, :, 0:2], fst[:, fg, :, :], eng="pool")
                    elif tl["first"]:
                        P.memset(ge[:, :, 0:2], 0.0, eng="pool")
                    else:
                        P.copy(ge[:, 0, 0:2], fcarry[:, fg, :], eng="pool")
                    P.act(ac, ge[:, :, 0:T], AF.Identity, bias=fcbT[:, l, fg:fg + 1],
                          scale=fcwT[:, l, fg, 0:1])
                    P.stt(ac, ge[:, :, 1:1 + T], fcwT[:, l, fg, 1:2], ac, ALU.mult, ALU.add)
                    P.stt(ac, ge[:, :, 2:2 + T], fcwT[:, l, fg, 2:3], ac, ALU.mult, ALU.add)
                    if samp:
                        P.copy(fout[:, fg, :, :], ge[:, :, T:T + 2], eng="pool")
                    else:
                        P.copy(fcarry[:, fg, :], ge[:, 0, T:T + 2], eng="pool")
                    P.act(ac, ac, AF.Silu)
                    P.tt(mt[:, fi, :], ac.rearrange("p s t -> p (s t)"), psU, ALU.mult)
                for m in range(8):
                    ps = pb[4 + m % 4][:, 0:TT]
                    for fi in range(nf):
                        P.mm(ps, wd[:, fi, m * 128:(m + 1) * 128], mt[:, fi, :],
                             start=(fi == 0), stop=(fi == nf - 1))
                    x_update(l, 40, m, ps, tl)
        X_tmp[0] = None
        P.dma(fconv_s_o[l], fout)
        P.dma(fconv_p_o[l], fcarry[:])

    chk('ffn')
    for tl in seq_tiles(256):
        t0, TT = tl["t0"], tl["TT"]
        W.reset()
        sq = W.alloc([128, 8, TT], BF16)
        tmp = W.alloc([128, 8, TT])
        rs = W.alloc([128, TT])
        P.act(sq, x[:, :, t0:t0 + TT], AF.Square)
        ps = pb[0][:, 0:TT]
        for k in range(8):
            P.mm(ps, ones_b[:], sq[:, k, :], start=(k == 0), stop=(k == 7))
        P.act(rs, ps, AF.Sqrt, bias=EPS, scale=1.0 / D)
        P.recip(rs, rs)
        P.tt(tmp, x[:, :, t0:t0 + TT], bc(rs.unsqueeze(1), [128, 8, TT]), ALU.mult)
        P.tt(tmp, tmp, bc(fnormT[:].unsqueeze(2), [128, 8, TT]), ALU.mult)
        P.dma(yT_o.rearrange("(k p) t -> p k t", p=128)[:, :, t0:t0 + TT], tmp)


_CACHE = {}


def _prep_inputs(inp):
    f = lambda a: np.ascontiguousarray(np.asarray(a, dtype=np.float32))
    xp = f(inp["x_prompt"]); xs = f(inp["x_sample"])
    cp = f(inp["c_prompt"]); cs = f(inp["c_sample"])

    def fm(v, nch):
        v = f(v)
        return np.ascontiguousarray(np.swapaxes(v.reshape(v.shape[:-1] + (nch, 128)), -1, -2))

    shared = {
        "w_ada": f(inp["w_ada"]), "b_adaT": fm(inp["b_ada"], 48),
        "w_in": f(inp["w_in"]), "w_out": f(inp["w_out"]),
        "w_gate": f(inp["ffn_w_gate"]), "w_up": f(inp["ffn_w_up"]), "w_down": f(inp["ffn_w_down"]),
        "nmixT": fm(inp["norm_mix"], 8), "nffnT": fm(inp["norm_ffn"], 8),
        "fnormT": fm(inp["final_norm"], 8),
        "hlbT": np.ascontiguousarray(f(inp["hg_lower_bound"]).reshape(2, 2, 128).transpose(0, 2, 1)),
        "hgnT": np.ascontiguousarray(f(inp["hg_norm"]).T),
        "gcwT": np.ascontiguousarray(f(inp["gdn_conv_w"]).reshape(2, 4, 12, 128).transpose(0, 3, 2, 1)),
        "galog": f(inp["gdn_a_log"]), "gdtb": f(inp["gdn_dt_bias"]),
        "gnT": np.ascontiguousarray(f(inp["gdn_norm"]).T),
        "lcwT": np.ascontiguousarray(f(inp["lru_conv_w"]).reshape(2, 4, 2, 128).transpose(0, 3, 2, 1)),
        "lcbT": fm(inp["lru_conv_b"], 2),
        "lwa": f(inp["lru_w_a"]), "lwx": f(inp["lru_w_x"]),
        "lbaT": fm(inp["lru_b_a"], 2), "lbxT": fm(inp["lru_b_x"], 2), "llamT": fm(inp["lru_lambda"], 2),
        "fcwT": np.ascontiguousarray(f(inp["ffn_conv_w"]).reshape(2, 3, NFC, 128).transpose(0, 3, 2, 1)),
        "fcbT": fm(inp["ffn_conv_b"], NFC),
    }
    s_hg = f(inp["state_hgrn"]); s_gd = f(inp["state_gdn"]); s_gc = f(inp["state_gdn_conv"])
    s_lr = f(inp["state_lru"]); s_lc = f(inp["state_lru_conv"]); s_fc = f(inp["state_ffn_conv"])
    maps = []
    for i in range(NCORES):
        sl = slice(NSQ * i, NSQ * (i + 1))
        xT = np.concatenate([xp[i].T, xs[sl].reshape(NTS, D).T], axis=1)
        cT = np.concatenate([cp[i][:, None], cs[sl].T], axis=1)
        m = dict(shared)
        m["xT"] = np.ascontiguousarray(xT)
        m["cT"] = np.ascontiguousarray(cT)
        m["st_hg"] = np.ascontiguousarray(
            s_hg[:, sl].reshape(2, NSQ, 2, 2, 64, 64).transpose(0, 3, 4, 1, 2, 5).reshape(2, 128, NSQ, 2, 64))
        m["st_gdn"] = np.ascontiguousarray(s_gd[:, sl])
        m["st_gconv"] = np.ascontiguousarray(s_gc[:, sl].reshape(2, NSQ, 3, 12, 128).transpose(0, 4, 3, 1, 2))
        m["st_lru"] = np.ascontiguousarray(s_lr[:, sl].reshape(2, NSQ, 2, 128).transpose(0, 3, 2, 1))
        m["st_lconv"] = np.ascontiguousarray(s_lc[:, sl].reshape(2, NSQ, 3, 2, 128).transpose(0, 4, 3, 1, 2))
        m["st_fconv"] = np.ascontiguousarray(s_fc[:, sl].reshape(2, NSQ, 2, NFC, 128).transpose(0, 4, 3, 1, 2))
        maps.append(m)
    return maps


def kernel(**inputs):
    if "nc" not in _CACHE:
        _CACHE["nc"] = build_program()
    nc = _CACHE["nc"]
    maps = _prep_inputs(inputs)
    res = run_bass_kernel_spmd(nc, maps, core_ids=list(range(NCORES)))
    R = res.results
    B = NCORES
    y_p = np.empty((B, TP, D), np.float32)
    y_s = np.empty((B * NSQ, TSQ, D), np.float32)
    hg_p = np.empty((2, B, 4, 64, 64), np.float32)
    gdn_p = np.empty((2, B, 4, 128, 128), np.float32)
    gconv_p = np.empty((2, B, 3, 1536), np.float32)
    lru_p = np.empty((2, B, 256), np.float32)
    lconv_p = np.empty((2, B, 3, 256), np.float32)
    fconv_p = np.empty((2, B, 2, DFF), np.float32)
    hg_s = np.empty((2, B * NSQ, 4, 64, 64), np.float32)
    gdn_s = np.empty((2, B * NSQ, 4, 128, 128), np.float32)
    gconv_s = np.empty((2, B * NSQ, 3, 1536), np.float32)
    lru_s = np.empty((2, B * NSQ, 256), np.float32)
    lconv_s = np.empty((2, B * NSQ, 3, 256), np.float32)
    fconv_s = np.empty((2, B * NSQ, 2, DFF), np.float32)
    for i in range(B):
        r = R[i]
        sl = slice(NSQ * i, NSQ * (i + 1))
        yT = np.asarray(r["yT"])
        y_p[i] = yT[:, :TP].T
        y_s[sl] = yT[:, TP:].T.reshape(NSQ, TSQ, D)
        hg_p[:, i] = np.asarray(r["o_hg_p"]).reshape(2, 2, 64, 2, 64).transpose(0, 3, 1, 2, 4).reshape(2, 4, 64, 64)
        gdn_p[:, i] = np.asarray(r["o_gdn_p"])
        gconv_p[:, i] = np.asarray(r["o_gconv_p"]).transpose(0, 3, 2, 1).reshape(2, 3, 1536)
        lru_p[:, i] = np.asarray(r["o_lru_p"]).transpose(0, 2, 1).reshape(2, 256)
        lconv_p[:, i] = np.asarray(r["o_lconv_p"]).transpose(0, 3, 2, 1).reshape(2, 3, 256)
        fconv_p[:, i] = np.asarray(r["o_fconv_p"]).transpose(0, 3, 2, 1).reshape(2, 2, DFF)
        hg_s[:, sl] = np.asarray(r["o_hg_s"]).reshape(2, 2, 64, NSQ, 2, 64).transpose(0, 3, 4, 1, 2, 5).reshape(2, NSQ, 4, 64, 64)
        gdn_s[:, sl] = np.asarray(r["o_gdn_s"])
        gconv_s[:, sl] = np.asarray(r["o_gconv_s"]).transpose(0, 3, 4, 2, 1).reshape(2, NSQ, 3, 1536)
        lru_s[:, sl] = np.asarray(r["o_lru_s"]).transpose(0, 3, 2, 1).reshape(2, NSQ, 256)
        lconv_s[:, sl] = np.asarray(r["o_lconv_s"]).transpose(0, 3, 4, 2, 1).reshape(2, NSQ, 3, 256)
        fconv_s[:, sl] = np.asarray(r["o_fconv_s"]).transpose(0, 3, 4, 2, 1).reshape(2, NSQ, 2, DFF)
    return (y_p, y_s, hg_p, gdn_p, gconv_p, lru_p, lconv_p, fconv_p,
            hg_s, gdn_s, gconv_s, lru_s, lconv_s, fconv_s)
```
